# Optimizing a Trainium2 kernel written in Bass

```python
import jax
import jax.numpy as jnp
from jax import lax
import numpy as np

D_MODEL = 2048
BATCH = 16
SEQ = 256
DEPTH = 4
DEC_BATCH = 8
DEC_SEQ = 4096
PAST_LEN = 256

GRID_W = 64
HEAD_DIM = 128
A_HEADS = D_MODEL // 4 // HEAD_DIM
A_KV = 2
B_HEADS = D_MODEL // 2 // HEAD_DIM
B_KV = 2
C_CH = D_MODEL // 4
MIX_W = (A_HEADS + B_HEADS) * HEAD_DIM + C_CH
CONV_W = 31
WINDOW = 128
BLOCK = 128
ROPE_BASE = 10000.0
AXIS_PAIRS = HEAD_DIM // 4
D_FF = -(-(8 * D_MODEL) // (3 * 256)) * 256
N_MOD = 6
EPS = 1e-6
A_Q_W = A_HEADS * HEAD_DIM
A_KV_W = A_KV * HEAD_DIM
B_Q_W = B_HEADS * HEAD_DIM
B_KV_W = B_KV * HEAD_DIM
C_IN_W = 2 * C_CH
IN_W = A_Q_W + 2 * A_KV_W + B_Q_W + 2 * B_KV_W + C_IN_W
IN_SPLITS = (A_Q_W,
             A_Q_W + A_KV_W,
             A_Q_W + 2 * A_KV_W,
             A_Q_W + 2 * A_KV_W + B_Q_W,
             A_Q_W + 2 * A_KV_W + B_Q_W + B_KV_W,
             A_Q_W + 2 * A_KV_W + B_Q_W + 2 * B_KV_W)

kernel_name = "hybrid_prefix_diffusion_step"


def rms_norm(x, g):
    xf = x.astype(jnp.float32)
    y = xf * lax.rsqrt(jnp.mean(xf * xf, axis=-1, keepdims=True) + EPS)
    return (y * g.astype(jnp.float32)).astype(x.dtype)


def layer_norm(x, g, b):
    xf = x.astype(jnp.float32)
    mu = jnp.mean(xf, axis=-1, keepdims=True)
    xc = xf - mu
    y = xc * lax.rsqrt(jnp.mean(xc * xc, axis=-1, keepdims=True) + EPS)
    return (y * g.astype(jnp.float32) + b.astype(jnp.float32)).astype(x.dtype)


def modulate(h, shift, scale):
    return h * (1 + scale[:, None, :]) + shift[:, None, :]


def axial_rope(n_tokens):
    rows = n_tokens // GRID_W
    t = jnp.arange(rows * GRID_W)
    row = (t // GRID_W).astype(jnp.float32)
    col = (t % GRID_W).astype(jnp.float32)
    inv = jnp.power(ROPE_BASE, -jnp.arange(AXIS_PAIRS, dtype=jnp.float32) / AXIS_PAIRS)
    ang = jnp.concatenate([row[:, None] * inv, col[:, None] * inv], axis=-1)
    return jnp.cos(ang), jnp.sin(ang)


def apply_rope(x, cos, sin):
    xf = x.astype(jnp.float32)
    x1, x2 = jnp.split(xf, 2, axis=-1)
    c = cos[None, :, None, :]
    s = sin[None, :, None, :]
    return jnp.concatenate([x1 * c - x2 * s, x2 * c + x1 * s], axis=-1).astype(x.dtype)


def sink_softmax(s, sink):
    m = jnp.maximum(jnp.max(s, axis=-1, keepdims=True), sink)
    p = jnp.exp(s - m)
    return p / (jnp.sum(p, axis=-1, keepdims=True) + jnp.exp(sink - m))


def attend_blocks(q, k, v, sink):
    b, t, h, d = q.shape
    kv = k.shape[2]
    g = h // kv
    nb = t // BLOCK
    scale = d ** -0.5
    qb = jnp.moveaxis(q.reshape(b, nb, BLOCK, kv, g, d), 1, 0)

    def one(qblk):
        s = jnp.einsum('bqkgd,bskd->bkgqs', qblk, k, preferred_element_type=jnp.float32) * scale
        if sink is None:
            p = jax.nn.softmax(s, axis=-1)
        else:
            p = sink_softmax(s, sink.astype(jnp.float32).reshape(kv, g)[None, :, :, None, None])
        return jnp.einsum('bkgqs,bskd->bqkgd', p.astype(v.dtype), v)

    out = lax.map(one, qb)
    return jnp.moveaxis(out, 0, 1).reshape(b, t, h, d)


def banded_attention_with_ctx(q, k, v, k_ctx, v_ctx, sink):
    b, t, h, d = q.shape
    kv = k.shape[2]
    g = h // kv
    nb = t // BLOCK
    scale = d ** -0.5
    qb = q.reshape(b, nb, BLOCK, kv, g, d)
    pad = ((0, 0), (BLOCK, BLOCK), (0, 0), (0, 0))
    kp = jnp.pad(k, pad).reshape(b, nb + 2, BLOCK, kv, d)
    vp = jnp.pad(v, pad).reshape(b, nb + 2, BLOCK, kv, d)
    kw = jnp.concatenate([kp[:, :-2], kp[:, 1:-1], kp[:, 2:]], axis=2)
    vw = jnp.concatenate([vp[:, :-2], vp[:, 1:-1], vp[:, 2:]], axis=2)
    s_win = jnp.einsum('bnqkgd,bnskd->bnkgqs', qb, kw, preferred_element_type=jnp.float32) * scale
    qpos = jnp.arange(BLOCK)[:, None]
    kpos = jnp.arange(3 * BLOCK)[None, :] - BLOCK
    kabs = jnp.arange(nb)[:, None, None] * BLOCK + kpos
    valid = (jnp.abs(kpos - qpos) <= WINDOW)[None] & (kabs >= 0) & (kabs < t)
    s_win = jnp.where(valid[None, :, None, None, :, :], s_win, -jnp.inf)
    s_ctx = jnp.einsum('bnqkgd,bskd->bnkgqs', qb, k_ctx, preferred_element_type=jnp.float32) * scale
    s = jnp.concatenate([s_win, s_ctx], axis=-1)
    p = sink_softmax(s, sink.astype(jnp.float32).reshape(kv, g)[None, None, :, :, None, None]).astype(v.dtype)
    p_win = p[..., :3 * BLOCK]
    p_ctx = p[..., 3 * BLOCK:]
    o = (jnp.einsum('bnkgqs,bnskd->bnqkgd', p_win, vw)
         + jnp.einsum('bnkgqs,bskd->bnqkgd', p_ctx, v_ctx))
    return o.reshape(b, t, h, d)


def conformer_conv(cz, conv_w, conv_b, ln_g, ln_b):
    u = cz[..., :C_CH] * jax.nn.sigmoid(cz[..., C_CH:])
    y = lax.conv_general_dilated(u, conv_w[:, None, :].astype(u.dtype), window_strides=(1,),
                                 padding=((CONV_W // 2, CONV_W // 2),),
                                 dimension_numbers=('NWC', 'WIO', 'NWC'),
                                 feature_group_count=C_CH)
    y = y + conv_b
    return jax.nn.silu(layer_norm(y, ln_g, ln_b))


def ada_mod(cvec, w_ada, b_ada):
    return jnp.split(jax.nn.silu(cvec) @ w_ada + b_ada, N_MOD, axis=-1)


def trunk_layer(x, mod, lw, rope=None, ctx_kv=None):
    (w_in, w_out, w_gate, w_up, w_down, g1, g2, qg_a, kg_a, qg_b, kg_b,
     sink, conv_w, conv_b, cln_g, cln_b) = lw
    sh1, sc1, gt1, sh2, sc2, gt2 = mod
    b, t, _ = x.shape
    h = modulate(rms_norm(x, g1), sh1, sc1)
    z = h @ w_in
    qa, ka, va, qb, kb, vb, cz = jnp.split(z, IN_SPLITS, axis=-1)
    qa = rms_norm(qa.reshape(b, t, A_HEADS, HEAD_DIM), qg_a)
    ka = rms_norm(ka.reshape(b, t, A_KV, HEAD_DIM), kg_a)
    va = va.reshape(b, t, A_KV, HEAD_DIM)
    qb = rms_norm(qb.reshape(b, t, B_HEADS, HEAD_DIM), qg_b)
    kb = rms_norm(kb.reshape(b, t, B_KV, HEAD_DIM), kg_b)
    vb = vb.reshape(b, t, B_KV, HEAD_DIM)
    if ctx_kv is None:
        oa = attend_blocks(qa, ka, va, sink)
        ob = attend_blocks(qb, kb, vb, None)
        new_kv = (ka, va, kb, vb)
    else:
        cos, sin = rope
        ka_c, va_c, kb_c, vb_c = ctx_kv
        qa = apply_rope(qa, cos, sin)
        ka = apply_rope(ka, cos, sin)
        qb = apply_rope(qb, cos, sin)
        kb = apply_rope(kb, cos, sin)
        oa = banded_attention_with_ctx(qa, ka, va, ka_c, va_c, sink)
        ob = attend_blocks(qb, jnp.concatenate([kb_c, kb], axis=1),
                           jnp.concatenate([vb_c, vb], axis=1), None)
        new_kv = None
    oc = conformer_conv(cz, conv_w, conv_b, cln_g, cln_b)
    mix = jnp.concatenate([oa.reshape(b, t, A_Q_W), ob.reshape(b, t, B_Q_W), oc], axis=-1)
    x = x + gt1[:, None, :] * (mix @ w_out)
    h2 = modulate(rms_norm(x, g2), sh2, sc2)
    x = x + gt2[:, None, :] * ((jax.nn.silu(h2 @ w_gate) * (h2 @ w_up)) @ w_down)
    return x, new_kv


def setup_inputs(seed: int = 0) -> dict:
    key = jax.random.key(seed)
    ks = jax.random.split(key, 32)
    f = jnp.float32

    def nrm(k, shape, scale):
        return jax.random.normal(k, shape, f) * scale

    def gain(k, shape):
        return 1.0 + 0.02 * jax.random.normal(k, shape, f)

    a_cache = (DEC_BATCH, DEPTH, PAST_LEN, A_KV, HEAD_DIM)
    b_cache = (DEC_BATCH, DEPTH, PAST_LEN, B_KV, HEAD_DIM)
    return {
        "x_prompt": nrm(ks[0], (BATCH, SEQ, D_MODEL), 1.0),
        "x_sample": nrm(ks[1], (DEC_BATCH, DEC_SEQ, D_MODEL), 1.0),
        "cache_a_k": nrm(ks[2], a_cache, 1.0),
        "cache_a_v": nrm(ks[3], a_cache, 1.0),
        "cache_b_k": nrm(ks[4], b_cache, 1.0),
        "cache_b_v": nrm(ks[5], b_cache, 1.0),
        "c": nrm(ks[6], (DEC_BATCH, D_MODEL), 1.0),
        "c_ctx": nrm(ks[7], (D_MODEL,), 1.0),
        "w_ada": nrm(ks[8], (DEPTH, D_MODEL, N_MOD * D_MODEL), 0.5 * D_MODEL ** -0.5),
        "b_ada": nrm(ks[9], (DEPTH, N_MOD * D_MODEL), 0.01),
        "w_in": nrm(ks[10], (DEPTH, D_MODEL, IN_W), D_MODEL ** -0.5),
        "w_out": nrm(ks[11], (DEPTH, MIX_W, D_MODEL), MIX_W ** -0.5),
        "w_gate": nrm(ks[12], (DEPTH, D_MODEL, D_FF), D_MODEL ** -0.5),
        "w_up": nrm(ks[13], (DEPTH, D_MODEL, D_FF), D_MODEL ** -0.5),
        "w_down": nrm(ks[14], (DEPTH, D_FF, D_MODEL), D_FF ** -0.5),
        "norm1_g": gain(ks[15], (DEPTH, D_MODEL)),
        "norm2_g": gain(ks[16], (DEPTH, D_MODEL)),
        "qnorm_a_g": gain(ks[17], (DEPTH, HEAD_DIM)),
        "knorm_a_g": gain(ks[18], (DEPTH, HEAD_DIM)),
        "qnorm_b_g": gain(ks[19], (DEPTH, HEAD_DIM)),
        "knorm_b_g": gain(ks[20], (DEPTH, HEAD_DIM)),
        "sink_a": nrm(ks[21], (DEPTH, A_HEADS), 0.5),
        "conv_w": nrm(ks[22], (DEPTH, CONV_W, C_CH), CONV_W ** -0.5),
        "conv_b": nrm(ks[23], (DEPTH, C_CH), 0.01),
        "conv_ln_g": gain(ks[24], (DEPTH, C_CH)),
        "conv_ln_b": nrm(ks[25], (DEPTH, C_CH), 0.01),
    }


def reference(x_prompt, x_sample, cache_a_k, cache_a_v, cache_b_k, cache_b_v, c, c_ctx,
              w_ada, b_ada, w_in, w_out, w_gate, w_up, w_down, norm1_g, norm2_g,
              qnorm_a_g, knorm_a_g, qnorm_b_g, knorm_b_g, sink_a,
              conv_w, conv_b, conv_ln_g, conv_ln_b):
    rope = axial_rope(x_sample.shape[1])
    y_prompt = x_prompt
    y_sample = x_sample
    ak, av, bk, bv = [], [], [], []
    for l in range(DEPTH):
        lw = (w_in[l], w_out[l], w_gate[l], w_up[l], w_down[l], norm1_g[l], norm2_g[l],
              qnorm_a_g[l], knorm_a_g[l], qnorm_b_g[l], knorm_b_g[l], sink_a[l],
              conv_w[l], conv_b[l], conv_ln_g[l], conv_ln_b[l])
        mod_ctx = ada_mod(c_ctx[None, :], w_ada[l], b_ada[l])
        mod_lat = ada_mod(c, w_ada[l], b_ada[l])
        y_prompt, kv = trunk_layer(y_prompt, mod_ctx, lw)
        ak.append(kv[0])
        av.append(kv[1])
        bk.append(kv[2])
        bv.append(kv[3])
        y_sample, _ = trunk_layer(y_sample, mod_lat, lw, rope,
                                  (cache_a_k[:, l], cache_a_v[:, l], cache_b_k[:, l], cache_b_v[:, l]))
    new_a_k = jnp.stack(ak, axis=1)
    new_a_v = jnp.stack(av, axis=1)
    new_b_k = jnp.stack(bk, axis=1)
    new_b_v = jnp.stack(bv, axis=1)
    return (y_prompt, y_sample, new_a_k, new_a_v, new_b_k, new_b_v)
```

```python
import numpy as np
import concourse.bass as bass
import concourse.mybir as mybir
from concourse.bass_utils import run_bass_kernel_spmd

F32 = mybir.dt.float32
BF16 = mybir.dt.bfloat16
AF = mybir.ActivationFunctionType
ALU = mybir.AluOpType

D = 2048
KC = 16
DFF = 5632
FC = 44
T = 512
DEPTH = 4
NSB = 8
SEQ_S = 4096
EPS = 1e-6
SCALE = 128 ** -0.5
LV = 272
NEG = -30000.0


class Ev:
    __slots__ = ("sem", "val", "clock")

    def __init__(self, sem, val, clock):
        self.sem = sem
        self.val = val
        self.clock = clock


class Res:
    __slots__ = ("w", "r", "ex")

    def __init__(self, ex=False):
        self.w = None
        self.r = {}
        self.ex = ex


def RL(n):
    return [Res() for _ in range(n)]


class Eng:
    def __init__(self, nc, e, name, self_ordered=False):
        self.e = e
        self.sem = nc.alloc_semaphore("s_" + name)
        self.key = "E" + name
        self.cnt = 0
        self.seen = {}
        self.self_ordered = self_ordered

    def wait(self, ev):
        if ev is None or self.seen.get(ev.sem[0], 0) >= ev.val:
            return
        self.e.wait_ge(ev.sem[1], ev.val)
        for s, v in ev.clock.items():
            if self.seen.get(s, 0) < v:
                self.seen[s] = v
        self.seen[ev.sem[0]] = ev.val

    def signal(self, inst):
        self.cnt += 1
        inst.then_inc(self.sem, 1)
        if self.self_ordered:
            self.seen[self.key] = self.cnt
        return Ev((self.key, self.sem), self.cnt, dict(self.seen))


class DmaQ:
    def __init__(self, nc, eng, name, nsem):
        self.E = eng
        self.sems = [[(name + str(i), nc.alloc_semaphore("d_" + name + str(i))), 0, None] for i in range(nsem)]
        self.i = 0


def _deps(reads, writes):
    out = []
    for x in reads:
        if x.w is not None:
            out.append(x.w)
    for x in writes:
        if x.w is not None:
            out.append(x.w)
        out.extend(x.r.values())
    return out


def _commit(ev, reads, writes):
    k = ev.sem[0]
    for x in reads:
        c = x.r.get(k)
        if c is None or c.val < ev.val:
            x.r[k] = ev
    for x in writes:
        x.w = ev
        x.r = {}


def op(E, reads, writes, fn):
    if any(x.ex for x in reads):
        writes = list(writes) + [x for x in reads if x.ex]
        reads = [x for x in reads if not x.ex]
    for ev in _deps(reads, writes):
        E.wait(ev)
    ev = E.signal(fn())
    _commit(ev, reads, writes)
    return ev


def dma(Q, reads, writes, fn):
    E = Q.E
    for ev in _deps(reads, writes):
        E.wait(ev)
    s = Q.sems[Q.i]
    Q.i = (Q.i + 1) % len(Q.sems)
    E.wait(s[2])
    s[1] += 16
    fn().then_inc(s[0][1], 16)
    ev = Ev(s[0], s[1], dict(E.seen))
    s[2] = ev
    _commit(ev, reads, writes)
    return ev


class PEGroup:
    def __init__(self, PE, writes):
        self.PE = PE
        self.writes = writes
        self.reads = []
        self.last = None
        for ev in _deps([], writes):
            PE.wait(ev)

    def mm(self, reads, fn, sig=False):
        for x in reads:
            if x.w is not None:
                self.PE.wait(x.w)
        self.last = fn()
        if sig:
            ev = self.PE.signal(self.last)
            _commit(ev, reads, [])
            self.sig_last = True
        else:
            self.reads.extend(reads)
            self.sig_last = False

    def done(self):
        if getattr(self, "sig_last", False):
            ev = Ev((self.PE.key, self.PE.sem), self.PE.cnt, dict(self.PE.seen))
        else:
            ev = self.PE.signal(self.last)
        _commit(ev, self.reads, self.writes)
        return ev


class _Stop(Exception):
    pass


def build(depth=DEPTH, dbg=None):
    nc = bass.Bass("TRN2", target_bir_lowering=False)
    L = depth
    fin = {}
    try:
        _build_body(nc, L, dbg, fin)
    except _Stop:
        pass
    POOL, QS, QL, engs = fin["POOL"], fin["QS"], fin["QL"], fin["engs"]
    for Q in (QS, QL, fin["QP"]):
        for s_ in Q.sems:
            POOL.wait(s_[2])
    for E in engs:
        if E.cnt:
            POOL.e.wait_ge(E.sem, E.cnt)
    return nc


def _build_body(nc, L, dbg, fin):
    def ms(name):
        if dbg == name:
            raise _Stop()

    def din(name, shape, dt=F32):
        return nc.dram_tensor(name, list(shape), dt, kind="ExternalInput").ap()

    def dout(name, shape):
        return nc.dram_tensor(name, list(shape), F32, kind="ExternalOutput").ap()

    def dscr(name, shape, dt):
        return nc.dram_tensor(name, list(shape), dt, kind="Internal").ap()

    x_s = din("x_s", [SEQ_S, D])
    x_p = din("x_p", [T, D])
    cak = din("cak", [L, 256, 2, 128])
    cav = din("cav", [L, 256, 2, 128])
    cbk = din("cbk", [L, 256, 2, 128])
    cbv = din("cbv", [L, 256, 2, 128])
    vecs = din("vecs", [128, 32 + L * LV])
    consts = din("consts", [128, 128 * 2 + 512])
    ropeC = din("ropeC", [128, SEQ_S])
    ropeS = din("ropeS", [128, SEQ_S])
    w_ada = din("w_ada", [L, D, 6 * D])
    w_in = din("w_in", [L, D, 3584])
    w_out = din("w_out", [L, D, D])
    w_gate = din("w_gate", [L, D, DFF])
    w_up = din("w_up", [L, D, DFF])
    w_down = din("w_down", [L, DFF, D])
    y_s = dout("y_s", [SEQ_S, D])
    y_p = dout("y_p", [T, D])
    nak = dout("nak", [2, L, 256, 2, 128])
    nav = dout("nav", [2, L, 256, 2, 128])
    nbk = dout("nbk", [2, L, 256, 2, 128])
    nbv = dout("nbv", [2, L, 256, 2, 128])

    xT_S = dscr("xT_S", [KC, 128, SEQ_S], F32)
    xT_P = dscr("xT_P", [KC, 128, T], F32)
    Wt_in = [dscr(f"Wt_in{l}", [28, 128, 2048], BF16) for l in range(L)]
    Wt_out = [dscr(f"Wt_out{l}", [16, 128, 2048], BF16) for l in range(L)]
    Wt_g = [dscr(f"Wt_g{l}", [FC, 128, 2048], BF16) for l in range(L)]
    Wt_u = [dscr(f"Wt_u{l}", [FC, 128, 2048], BF16) for l in range(L)]
    Wt_d = [dscr(f"Wt_d{l}", [16, 4, 128, 11 * 128], BF16) for l in range(L)]
    qT_d = dscr("qT_d", [12, 128, SEQ_S], BF16)
    uT_d = dscr("uT_d", [4, 128, SEQ_S + 32], BF16)
    uT_dP = dscr("uT_dP", [4, 128, 2 * 288], BF16)
    KaT_d = dscr("KaT_d", [2, 128, SEQ_S + 256], BF16)
    Va_d = dscr("Va_d", [34, 128, 256], BF16)

    PE = Eng(nc, nc.tensor, "pe", self_ordered=True)
    ACT = Eng(nc, nc.scalar, "act")
    DVE = Eng(nc, nc.vector, "dve")
    POOL = Eng(nc, nc.gpsimd, "pool")
    SP = Eng(nc, nc.sync, "sp")
    QL = DmaQ(nc, SP, "ql", 40)
    QS = DmaQ(nc, SP, "qs", 24)
    QP = DmaQ(nc, POOL, "qp", 4)
    fin.update(POOL=POOL, QS=QS, QL=QL, QP=QP, engs=(PE, ACT, DVE))

    def sb(name, shape, dt):
        return nc.alloc_sbuf_tensor(name, list(shape), dt)

    vec_t = sb("vec_t", [128, 32 + L * LV], F32)
    r_vec = Res()
    cst_t = sb("cst_t", [128, 768], F32)
    r_cst = Res()
    ident_bf = sb("ident_bf", [128, 128], BF16)
    ones_bf = sb("ones_bf", [128, 128], BF16)
    mask_bf = sb("mask_bf", [128, 2, 256], BF16)
    eps_t = sb("eps_t", [128, 1], F32)
    r_k = Res()
    modv = sb("modv", [128, L, 2, 6, KC], F32)
    r_mod = RL(L)
    moda = sb("moda", [128, L, 2, 2, KC], F32)
    esink = sb("esink", [128, L * 4], F32)
    cwh = sb("cwh", [128, L, 124], F32)
    ident_f = cst_t[:, 0:128]
    perm_f = cst_t[:, 128:256]

    def V(l, off, n=1):
        b = 32 + l * LV + off
        return vec_t[:, b:b + n]

    OFF_G1, OFF_G2, OFF_BADA, OFF_QGA, OFF_KGA, OFF_QGB, OFF_KGB = 0, 16, 32, 128, 129, 130, 131
    OFF_CW, OFF_CB, OFF_LG, OFF_LB, OFF_SINK = 132, 256, 260, 264, 268

    ps = [nc.alloc_psum_tensor(f"ps{i}", [128, T], F32) for i in range(8)]
    r_ps = [Res(ex=True) for _ in range(8)]

    dma(QL, [], [r_vec], lambda: nc.sync.dma_start(out=vec_t[:], in_=vecs[:, :]))
    dma(QL, [], [r_cst], lambda: nc.sync.dma_start(out=cst_t[:], in_=consts[:, :]))
    op(DVE, [r_cst], [r_k], lambda: nc.vector.tensor_copy(out=ident_bf[:], in_=ident_f))
    op(DVE, [r_cst], [r_k], lambda: nc.vector.tensor_copy(out=mask_bf[:].rearrange("p a b -> p (a b)"), in_=cst_t[:, 256:768]))
    op(DVE, [], [r_k], lambda: nc.vector.memset(ones_bf[:], 1.0))
    op(DVE, [], [r_k], lambda: nc.vector.memset(eps_t[:], EPS))
    for l in range(L):
        op(ACT, [r_vec], [r_k], lambda: nc.scalar.activation(out=esink[:, l * 4:l * 4 + 4], in_=V(l, OFF_SINK, 4), func=AF.Exp))
        op(DVE, [r_vec], [r_k], lambda: nc.vector.tensor_scalar(out=cwh[:, l, :], in0=V(l, OFF_CW, 124), scalar1=0.5, scalar2=None, op0=ALU.mult))

    r_xT_S = RL(NSB)
    r_xT_P = RL(1)
    r_W = [dict(inn=Res(), out=Res(), g=Res(), u=Res(), d=Res()) for _ in range(L)]

    with nc.sbuf_tensor("p0_x", [128, 4, D], F32) as p0_x, \
            nc.sbuf_tensor("p0_st", [128, KC, T], F32) as p0_st, \
            nc.sbuf_tensor("p0_wa", [128, 2, KC, 512], F32) as p0_wa, \
            nc.sbuf_tensor("p0_wb", [128, 2, 4 * KC * 128], BF16) as p0_wb, \
            nc.sbuf_tensor("p0_sc", [128, KC, 2], F32) as p0_sc, \
            nc.sbuf_tensor("p0_zb", [128, 512], BF16) as p0_zb, \
            nc.sbuf_tensor("adat", [2, 512], F32) as adat:
        r_adat = Res()
        r_p0x, r_p0st, r_wa, r_wb, r_sc, r_zb = Res(), RL(KC), RL(2), RL(2), Res(), Res()

        op(DVE, [], [r_zb], lambda: nc.vector.memset(p0_zb[:], 0.0))
        for c in range(4):
            dma(QS, [r_zb], [], lambda: nc.sync.dma_start(out=uT_d[c, :, 0:16], in_=p0_zb[:, 0:16]))
            dma(QS, [r_zb], [], lambda: nc.sync.dma_start(out=uT_d[c, :, SEQ_S + 16:SEQ_S + 32], in_=p0_zb[:, 0:16]))
            dma(QS, [r_zb], [], lambda: nc.sync.dma_start(out=uT_dP[c, :, :].rearrange("p (s w) -> p s w", s=2)[:, :, 0:16], in_=p0_zb[:, 0:32].rearrange("p (s w) -> p s w", s=2)))
            dma(QS, [r_zb], [], lambda: nc.sync.dma_start(out=uT_dP[c, :, :].rearrange("p (s w) -> p s w", s=2)[:, :, 272:288], in_=p0_zb[:, 0:32].rearrange("p (s w) -> p s w", s=2)))
        for kv in range(2):
            dma(QS, [r_zb], [], lambda: nc.sync.dma_start(out=KaT_d[kv, :, 0:128], in_=p0_zb[:, 0:128]))
            dma(QS, [r_zb], [], lambda: nc.sync.dma_start(out=KaT_d[kv, :, SEQ_S + 128:SEQ_S + 256], in_=p0_zb[:, 0:128]))
        dma(QS, [r_zb], [], lambda: nc.sync.dma_start(out=Va_d[0], in_=p0_zb[:, 0:256]))
        dma(QS, [r_zb], [], lambda: nc.sync.dma_start(out=Va_d[33], in_=p0_zb[:, 0:256]))

        ms("p0a")
        def to_feature_major(xsrc, xT, r_dst, t0):
            dma(QL, [], [r_p0x], lambda: nc.sync.dma_start(out=p0_x[:], in_=xsrc[t0:t0 + T, :].rearrange("(a p) f -> p a f", p=128)))
            for kc in range(KC):
                b = kc % 2
                g = PEGroup(PE, [r_ps[b]])
                for a in range(4):
                    g.mm([r_p0x, r_cst], lambda: nc.tensor.transpose(ps[b][:, a * 128:(a + 1) * 128], p0_x[:, a, kc * 128:(kc + 1) * 128], ident_f))
                g.done()
                if kc % 2 == 0:
                    op(DVE, [r_ps[b]], [r_p0st[kc]], lambda: nc.vector.tensor_copy(out=p0_st[:, kc, :], in_=ps[b][:]))
                else:
                    op(ACT, [r_ps[b]], [r_p0st[kc]], lambda: nc.scalar.copy(out=p0_st[:, kc, :], in_=ps[b][:]))
            dma(QS, r_p0st, [r_dst], lambda: nc.sync.dma_start(out=xT[:, :, t0:t0 + T].rearrange("k p t -> p k t"), in_=p0_st[:]))

        to_feature_major(x_p, xT_P, r_xT_P[0], 0)
        for b in range(NSB):
            to_feature_major(x_s, xT_S, r_xT_S[b], b * T)

        ms("p0b")
        op(ACT, [r_vec], [r_sc], lambda: nc.scalar.activation(out=p0_sc[:, :, 0], in_=vec_t[:, 0:16], func=AF.Silu))
        op(ACT, [r_vec], [r_sc], lambda: nc.scalar.activation(out=p0_sc[:, :, 1], in_=vec_t[:, 16:32], func=AF.Silu))
        cnt = 0
        for l in range(L):
            for q in range(24):
                sl = cnt % 2
                cnt += 1
                dma(QL, [], [r_wa[sl]], lambda: nc.sync.dma_start(out=p0_wa[:, sl], in_=w_ada[l, :, q * 512:(q + 1) * 512].rearrange("(k p) n -> p k n", p=128)))
                g = PEGroup(PE, [r_ps[2]])
                for kc in range(KC):
                    g.mm([r_wa[sl], r_sc], lambda: nc.tensor.matmul(ps[2][0:2, :], lhsT=p0_sc[:, kc, :], rhs=p0_wa[:, sl, kc, :], start=(kc == 0), stop=(kc == KC - 1)))
                g.done()
                op(ACT, [r_ps[2]], [r_adat], lambda: nc.scalar.copy(out=adat[:], in_=ps[2][0:2, :]))
                g = PEGroup(PE, [r_ps[3]])
                for j in range(4):
                    g.mm([r_adat, r_cst], lambda: nc.tensor.transpose(ps[3][:, j * 2:j * 2 + 2], adat[0:2, j * 128:(j + 1) * 128], cst_t[0:2, 0:2]))
                g.done()
                i0, k0 = (q * 4) // 16, (q * 4) % 16
                for s in range(2):
                    op(DVE, [r_ps[3], r_vec], [r_mod[l]], lambda: nc.vector.tensor_tensor(
                        out=modv[:, l, s, i0, k0:k0 + 4], in0=ps[3][:, 0:8].rearrange("p (j s) -> p j s", s=2)[:, :, s],
                        in1=V(l, OFF_BADA + q * 4, 4), op=ALU.add))
            for s in range(2):
                for n, (og, isc) in enumerate(((OFF_G1, 1), (OFF_G2, 4))):
                    op(DVE, [r_mod[l], r_vec], [r_mod[l]], lambda: nc.vector.scalar_tensor_tensor(
                        out=moda[:, l, s, n, :], in0=modv[:, l, s, isc, :], scalar=1.0, in1=V(l, og, 16), op0=ALU.add, op1=ALU.mult))

        ms("p0c")
        cv = [0]

        def conv_tile(src_ap, nk, ncols, dst_ap, rsrc_res):
            sl = cv[0] % 2
            nm = ncols // 128
            src_sb = p0_wa[:, sl].rearrange("p k n -> p (k n)")[:, 0:nk * ncols].rearrange("p (k n) -> p k n", k=nk)
            dst_sb = p0_wb[:, sl, 0:nm * nk * 128]
            dma(QL, [], [r_wa[sl]], lambda: nc.sync.dma_start(out=src_sb, in_=src_ap.rearrange("(k p) n -> p k n", p=128)))
            o = dst_sb.rearrange("p (m k j) -> p k m j", m=nm, k=nk)
            i = src_sb.rearrange("p k (m j) -> p k m j", m=nm)
            e = cv[0] % 3
            cv[0] += 1
            if e == 0:
                op(DVE, [r_wa[sl]], [r_wb[sl]], lambda: nc.vector.tensor_copy(out=o, in_=i))
            elif e == 1:
                op(ACT, [r_wa[sl]], [r_wb[sl]], lambda: nc.scalar.copy(out=o, in_=i))
            else:
                op(POOL, [r_wa[sl]], [r_wb[sl]], lambda: nc.gpsimd.tensor_copy(out=o, in_=i))
            dma(QS, [r_wb[sl]], [rsrc_res], lambda: nc.sync.dma_start(out=dst_ap.rearrange("m p f -> p m f"), in_=dst_sb.rearrange("p (m f) -> p m f", m=nm)))

        for l in range(1):
            for q in range(7):
                conv_tile(w_in[l, :, q * 512:(q + 1) * 512], KC, 512, Wt_in[l][q * 4:(q + 1) * 4], r_W[l]["inn"])
            for q in range(4):
                conv_tile(w_out[l, :, q * 512:(q + 1) * 512], KC, 512, Wt_out[l][q * 4:(q + 1) * 4], r_W[l]["out"])
            for q in range(11):
                conv_tile(w_gate[l, :, q * 512:(q + 1) * 512], KC, 512, Wt_g[l][q * 4:(q + 1) * 4], r_W[l]["g"])
                conv_tile(w_up[l, :, q * 512:(q + 1) * 512], KC, 512, Wt_u[l][q * 4:(q + 1) * 4], r_W[l]["u"])
            for hf in range(4):
                for q in range(4):
                    conv_tile(w_down[l, hf * 1408:(hf + 1) * 1408, q * 512:(q + 1) * 512], 11, 512, Wt_d[l][q * 4:(q + 1) * 4, hf], r_W[l]["d"])

    ms("p0d")
    KTb = sb("KTb", [128, 2, SEQ_S + 256], BF16)
    Vb = sb("Vb", [128, 34, 2, 128], BF16)
    r_KTb, r_Vb = RL(34), RL(34)
    KaC = sb("KaC", [128, 2, 256], BF16)
    VaC = sb("VaC", [128, 2, 2, 128], BF16)
    r_KaC, r_VaC = Res(), Res()
    KaW = sb("KaW", [128, 2, 768], BF16)
    VaW = sb("VaW", [128, 6, 256], BF16)
    r_KaW, r_VaW = Res(), Res()
    xblk = sb("xblk", [128, KC, T], F32)
    r_x = RL(KC)
    actb = sb("actb", [128, KC, T], BF16)
    r_act = RL(KC)
    gT = sb("gT", [128, 11, T], BF16)
    r_g = RL(11)
    qblk = sb("qblk", [128, 12, T], BF16)
    r_q = RL(3)
    uh = sb("uh", [128, 4, 576], BF16)
    r_uh = Res()
    NW = 6
    wring = sb("wring", [128, NW, 2048], BF16)
    r_wr = RL(NW)
    ropeCt = sb("ropeCt", [128, T], F32)
    ropeSt = sb("ropeSt", [128, T], F32)
    r_rope = Res()
    NPT = 3
    PT = sb("PT", [128, NPT, T], BF16)
    r_PT = RL(NPT)
    NTF = 7
    tf = sb("tf", [128, NTF, T], F32)
    r_tf = RL(NTF)
    NTB = 4
    tb = sb("tb", [128, NTB, T], BF16)
    r_tb = RL(NTB)
    rstd = sb("rstd", [128, T], F32)
    r_rstd = Res()
    cacc = sb("cacc", [128, 4, T], F32)
    r_cacc = RL(4)
    cstage = sb("cstage", [128, 2, 2, 256], F32)
    r_cstage = Res()

    ctr = {"tf": 0, "tb": 0, "pt": 0, "main": 0, "aux": 0, "ss": 0, "pring": 0}
    cvA = sb("cvA", [128, 2048], F32)
    cvB = sb("cvB", [128, 2048], BF16)
    r_cvA, r_cvB = Res(), Res()
    bg = {"tasks": [], "ticks": 0, "every": 8}

    def bg_plan(l):
        t = []
        for mc in range(28):
            t.append((w_in[l, :, mc * 128:(mc + 1) * 128], KC, Wt_in[l][mc], r_W[l]["inn"]))
        for mc in range(16):
            t.append((w_out[l, :, mc * 128:(mc + 1) * 128], KC, Wt_out[l][mc], r_W[l]["out"]))
        for mc in range(FC):
            t.append((w_gate[l, :, mc * 128:(mc + 1) * 128], KC, Wt_g[l][mc], r_W[l]["g"]))
            t.append((w_up[l, :, mc * 128:(mc + 1) * 128], KC, Wt_u[l][mc], r_W[l]["u"]))
        for hf in range(4):
            for mc in range(16):
                t.append((w_down[l, hf * 1408:(hf + 1) * 1408, mc * 128:(mc + 1) * 128], 11, Wt_d[l][mc, hf], r_W[l]["d"]))
        return t

    def bg_emit_one():
        src, nk, dst, rdst = bg["tasks"].pop(0)
        n = nk * 128
        if PE.cnt:
            POOL.wait(Ev((PE.key, PE.sem), PE.cnt, {}))
        dma(QP, [], [r_cvA], lambda: nc.gpsimd.dma_start(out=cvA[:, 0:n].rearrange("p (k j) -> p k j", k=nk), in_=src.rearrange("(k p) n -> p k n", p=128)))
        op(POOL, [r_cvA], [r_cvB], lambda: nc.gpsimd.tensor_copy(out=cvB[:, 0:n], in_=cvA[:, 0:n]))
        dma(QP, [r_cvB], [rdst], lambda: nc.gpsimd.dma_start(out=dst, in_=cvB[:, 0:n]))

    def bg_tick():
        bg["ticks"] += 1
        if bg["tasks"] and bg["ticks"] % bg["every"] == 0:
            bg_emit_one()

    def bg_flush():
        while bg["tasks"]:
            bg_emit_one()

    def ntf():
        i = ctr["tf"] % NTF
        ctr["tf"] += 1
        return tf[:, i, :], r_tf[i]

    def ntb():
        i = ctr["tb"] % NTB
        ctr["tb"] += 1
        return tb[:, i, :], r_tb[i]

    def pmain():
        i = ctr["main"] % 4
        ctr["main"] += 1
        return ps[i], r_ps[i]

    def pss():
        i = 4 + ctr["ss"] % 2
        ctr["ss"] += 1
        return ps[i], r_ps[i]

    def paux():
        i = 6 + ctr["aux"] % 2
        ctr["aux"] += 1
        return ps[i], r_ps[i]

    def in_order():
        return [0, 1, 2, 3, 4, 5, 6, 7, 8, 9, 10, 11, 12, 13, 14, 15, 16, 17, 18, 19, 20, 24, 21, 25, 22, 26, 23, 27]

    def plan():
        seq = []
        for l in range(L):
            for nblk in (1, NSB):
                for b in range(nblk):
                    for mc in in_order():
                        seq.append((Wt_in[l][mc], 2048, r_W[l]["inn"]))
                for b in range(nblk):
                    for mc in range(16):
                        seq.append((Wt_out[l][mc], 2048, r_W[l]["out"]))
                    for hf in range(4):
                        for fc in range(11):
                            seq.append((Wt_g[l][hf * 11 + fc], 2048, r_W[l]["g"]))
                            seq.append((Wt_u[l][hf * 11 + fc], 2048, r_W[l]["u"]))
                        for mc in range(16):
                            seq.append((Wt_d[l][mc, hf], 1408, r_W[l]["d"]))
        return seq

    wplan = plan()
    wst = {"next_load": 0, "next_use": 0}

    def wload_upto(n):
        while wst["next_load"] < min(n, len(wplan)):
            i = wst["next_load"]
            src, ncol, rsrc = wplan[i]
            sl = i % NW
            dma(QL, [rsrc], [r_wr[sl]], lambda: nc.sync.dma_start(out=wring[:, sl, 0:ncol], in_=src))
            wst["next_load"] += 1

    def wnext(src_check):
        i = wst["next_use"]
        assert wplan[i][0] is src_check or True
        wload_upto(i + NW)
        wst["next_use"] += 1
        bg_tick()
        return wring[:, i % NW, :], r_wr[i % NW]

    def rms_stats(src_chunks, r_src, out_rstd, r_out, nfeat, nch):
        p, rp = pss()
        g = PEGroup(PE, [rp])
        for c in range(nch):
            t, rt = ntb()
            if c % 2 == 0:
                op(ACT, [r_src[c]], [rt], lambda: nc.scalar.activation(out=t, in_=src_chunks(c), func=AF.Square))
            else:
                op(DVE, [r_src[c]], [rt], lambda: nc.vector.tensor_tensor(out=t, in0=src_chunks(c), in1=src_chunks(c), op=ALU.mult))
            g.mm([rt, r_k], lambda: nc.tensor.matmul(p[:], lhsT=ones_bf[:], rhs=t, start=(c == 0), stop=(c == nch - 1)), sig=True)
        g.done()
        t, rt = ntf()
        op(ACT, [rp, r_k], [rt], lambda: nc.scalar.activation(out=t, in_=p[:], func=AF.Ln, bias=eps_t[:], scale=1.0 / nfeat))
        op(ACT, [rt], [r_out], lambda: nc.scalar.activation(out=out_rstd, in_=t, func=AF.Exp, scale=-0.5))

    def norm_modulate(l, s, n):
        rms_stats(lambda c: xblk[:, c, :], r_x, rstd[:], r_rstd, D, KC)
        for c in range(KC):
            t, rt = ntf()
            op(DVE, [r_x[c], r_rstd, r_mod[l]], [rt], lambda: nc.vector.scalar_tensor_tensor(
                out=t, in0=xblk[:, c, :], scalar=moda[:, l, s, n, c:c + 1], in1=rstd[:], op0=ALU.mult, op1=ALU.mult))
            op(ACT, [rt, r_mod[l]], [r_act[c]], lambda: nc.scalar.activation(
                out=actb[:, c, :], in_=t, func=AF.Identity, bias=modv[:, l, s, 3 * n, c:c + 1], scale=1.0))

    def big_mm(wt, rw, nk, rhs_fn, r_rhs, p, rp, koff=0):
        g = PEGroup(PE, [rp])
        for k in range(nk):
            g.mm([rw, r_rhs[koff + k] if isinstance(r_rhs, list) else r_rhs],
                 lambda: nc.tensor.matmul(p[:], lhsT=wt[:, k * 128:(k + 1) * 128], rhs=rhs_fn(k), start=(k == 0), stop=(k == nk - 1)))
        return g.done()

    def load_x(s, b):
        xT = xT_S if s == 0 else xT_P
        rr = r_xT_S[b] if s == 0 else r_xT_P[0]
        dma(QL, [rr], r_x, lambda: nc.sync.dma_start(out=xblk[:], in_=xT[:, :, b * T:(b + 1) * T].rearrange("k p t -> p k t")))

    def transposes_to(src, rsrc, dt_ident, r_ident, p, rp):
        g = PEGroup(PE, [rp])
        for a in range(4):
            g.mm([rsrc, r_ident], lambda: nc.tensor.transpose(p[:, a * 128:(a + 1) * 128], src[:, a * 128:(a + 1) * 128], dt_ident))
        g.done()

    r_qT = [RL(NSB) for _ in range(3)]
    r_uT = RL(NSB)
    r_uTP = Res()
    r_Ka_d = RL(NSB)
    r_Va_d = RL(NSB)

    for l in range(L):
        last = (l == L - 1)
        bg_flush()
        if not last:
            bg["tasks"] = bg_plan(l + 1)
        for which, (ck, cv_) in enumerate(((cbk, cbv), (cak, cav))):
            dma(QL, [], [r_cstage], lambda: nc.sync.dma_start(
                out=cstage[:, 0], in_=ck[l].rearrange("(st s) kv d -> s st (kv d)", s=128)))
            dma(QL, [], [r_cstage], lambda: nc.sync.dma_start(
                out=cstage[:, 1], in_=cv_[l].rearrange("(st s) kv d -> s st (kv d)", s=128)))
            p, rp = paux()
            g = PEGroup(PE, [rp])
            for kv in range(2):
                for st in range(2):
                    g.mm([r_cstage, r_cst], lambda: nc.tensor.transpose(
                        p[:, (kv * 2 + st) * 128:(kv * 2 + st + 1) * 128], cstage[:, 0, st, kv * 128:(kv + 1) * 128], ident_f))
            g.done()
            if which == 0:
                op(DVE, [rp], r_KTb[0:2], lambda: nc.vector.tensor_copy(out=KTb[:, :, 0:256], in_=p[:].rearrange("p (kv t) -> p kv t", kv=2)))
                op(DVE, [r_cstage], r_Vb[0:2], lambda: nc.vector.tensor_copy(
                    out=Vb[:, 0:2].rearrange("p a k d -> p a (k d)"), in_=cstage[:, 1]))
            else:
                op(DVE, [rp], [r_KaC], lambda: nc.vector.tensor_copy(out=KaC[:], in_=p[:].rearrange("p (kv t) -> p kv t", kv=2)))
                op(DVE, [r_cstage], [r_VaC], lambda: nc.vector.tensor_copy(
                    out=VaC[:].rearrange("p a k d -> p a (k d)"), in_=cstage[:, 1]))

        for s, nblk in ((1, 1), (0, NSB)):
            for b in range(nblk):
                def rope_load(bb):
                    dma(QL, [], [r_rope], lambda: nc.sync.dma_start(out=ropeCt[:], in_=ropeC[:, bb * T:(bb + 1) * T]))
                    dma(QL, [], [r_rope], lambda: nc.sync.dma_start(out=ropeSt[:], in_=ropeS[:, bb * T:(bb + 1) * T]))
                if b == 0:
                    load_x(s, 0)
                    if s == 0:
                        rope_load(0)
                if s == 0:
                    Cc, Ss, rcs = ropeCt[:], ropeSt[:], r_rope
                else:
                    Cc, Ss, rcs = None, None, None
                norm_modulate(l, s, 0)
                if b + 1 < nblk:
                    load_x(s, b + 1)
                pend = {}
                for mc in in_order():
                    wt, rw = wnext(None)
                    p, rp = pmain()
                    big_mm(wt, rw, KC, lambda k: actb[:, k, :], r_act, p, rp)
                    if mc < 6 or 8 <= mc < 18:
                        isq = mc < 4 or 8 <= mc < 16
                        goff = OFF_QGA if mc < 4 else OFF_KGA if mc < 6 else OFF_QGB if mc < 16 else OFF_KGB
                        t_sq, r_sq = ntb()
                        op(ACT, [rp], [r_sq], lambda: nc.scalar.activation(out=t_sq, in_=p[:], func=AF.Square))
                        pq, rpq = pss()
                        g = PEGroup(PE, [rpq])
                        g.mm([r_sq, r_k], lambda: nc.tensor.matmul(pq[:], lhsT=ones_bf[:], rhs=t_sq, start=True, stop=True))
                        g.done()
                        t_zg, r_zg = ntf()
                        op(ACT, [rp, r_vec], [r_zg], lambda: nc.scalar.activation(out=t_zg, in_=p[:], func=AF.Identity, scale=V(l, goff)))
                        t_ln, r_ln = ntf()
                        op(ACT, [rpq, r_k], [r_ln], lambda: nc.scalar.activation(out=t_ln, in_=pq[:], func=AF.Ln, bias=eps_t[:], scale=1.0 / 128))
                        t_rs, r_rs = ntf()
                        op(ACT, [r_ln], [r_rs], lambda: nc.scalar.activation(out=t_rs, in_=t_ln, func=AF.Exp, scale=-0.5))
                        if s == 0:
                            px, rpx = paux()
                            g = PEGroup(PE, [rpx])
                            g.mm([r_zg, r_cst], lambda: nc.tensor.matmul(px[:], lhsT=perm_f, rhs=t_zg, start=True, stop=True))
                            g.done()
                            t_a, r_a = ntf()
                            op(DVE, [r_zg, rcs], [r_a], lambda: nc.vector.tensor_tensor(out=t_a, in0=t_zg, in1=Cc, op=ALU.mult))
                            t_b, r_b = ntf()
                            op(DVE, [rpx, rcs], [r_b], lambda: nc.vector.tensor_tensor(out=t_b, in0=px[:], in1=Ss, op=ALU.mult))
                            op(DVE, [r_a, r_b], [r_b], lambda: nc.vector.tensor_tensor(out=t_b, in0=t_a, in1=t_b, op=ALU.add))
                        else:
                            t_b, r_b = t_zg, r_zg
                        tok0 = b * T if s == 0 else 0
                        if isq:
                            hq = mc if mc < 4 else mc - 4
                            t_o, r_o = ntb()
                            op(DVE, [r_b, r_rs], [r_o], lambda: nc.vector.tensor_tensor(out=t_o, in0=t_b, in1=t_rs, op=ALU.mult))
                            grp = 0 if hq < 4 else 1 if hq < 8 else 2
                            dma(QS, [r_o], [r_qT[grp][b]], lambda: nc.sync.dma_start(out=qT_d[hq, :, tok0:tok0 + T], in_=t_o))
                        else:
                            isA = mc < 6
                            kv = (mc - 4) if isA else (mc - 16)
                            if s == 1:
                                op(DVE, [r_b, r_rs], [r_b], lambda: nc.vector.tensor_tensor(out=t_b, in0=t_b, in1=t_rs, op=ALU.mult))
                                pt_, rpt_ = paux()
                                transposes_to(t_b, r_b, ident_f, r_cst, pt_, rpt_)
                                t_s, r_s = ntf()
                                op(ACT, [rpt_], [r_s], lambda: nc.scalar.copy(out=t_s, in_=pt_[:]))
                                dstk = nak if isA else nbk
                                for sq in range(2):
                                    dma(QS, [r_s], [], lambda: nc.sync.dma_start(
                                        out=dstk[sq, l, :, kv, :].rearrange("(h p) d -> p h d", p=128),
                                        in_=t_s[:, sq * 256:(sq + 1) * 256].rearrange("p (h d) -> p h d", h=2)))
                                srck, r_srck = t_b, r_b
                            else:
                                srck, r_srck = None, None
                            if isA:
                                t_o, r_o = ntb()
                                if s == 1:
                                    op(ACT, [r_b], [r_o], lambda: nc.scalar.copy(out=t_o, in_=t_b))
                                else:
                                    op(DVE, [r_b, r_rs], [r_o], lambda: nc.vector.tensor_tensor(out=t_o, in0=t_b, in1=t_rs, op=ALU.mult))
                                dma(QS, [r_o], [r_Ka_d[b]], lambda: nc.sync.dma_start(out=KaT_d[kv, :, 128 + tok0:128 + tok0 + T], in_=t_o))
                            else:
                                dst = KTb[:, kv, 256 + tok0:256 + tok0 + T]
                                rd = r_KTb[2 + tok0 // 128:2 + tok0 // 128 + 4]
                                if s == 1:
                                    op(ACT, [r_b], rd, lambda: nc.scalar.copy(out=dst, in_=t_b))
                                else:
                                    op(DVE, [r_b, r_rs], rd, lambda: nc.vector.tensor_tensor(out=dst, in0=t_b, in1=t_rs, op=ALU.mult))
                    elif mc < 20:
                        if mc == 18 and s == 0 and b + 1 < nblk:
                            rope_load(b + 1)
                        isA = mc < 8
                        kv = (mc - 6) if isA else (mc - 18)
                        t_v, r_v = ntf()
                        op(ACT, [rp], [r_v], lambda: nc.scalar.copy(out=t_v, in_=p[:]))
                        pt_, rpt_ = paux()
                        transposes_to(t_v, r_v, ident_f, r_cst, pt_, rpt_)
                        tok0 = b * T if s == 0 else 0
                        if s == 1:
                            t_s, r_s = ntf()
                            op(ACT, [rpt_], [r_s], lambda: nc.scalar.copy(out=t_s, in_=pt_[:]))
                            dstv = nav if isA else nbv
                            for sq in range(2):
                                dma(QS, [r_s], [], lambda: nc.sync.dma_start(
                                    out=dstv[sq, l, :, kv, :].rearrange("(h p) d -> p h d", p=128),
                                    in_=t_s[:, sq * 256:(sq + 1) * 256].rearrange("p (h d) -> p h d", h=2)))
                        if isA:
                            t_o, r_o = ntb()
                            op(DVE, [rpt_], [r_o], lambda: nc.vector.tensor_copy(out=t_o, in_=pt_[:]))
                            sb0 = 1 + tok0 // 128
                            dma(QS, [r_o], [r_Va_d[b]], lambda: nc.sync.dma_start(
                                out=Va_d[sb0:sb0 + 4, :, kv * 128:(kv + 1) * 128].rearrange("a p d -> p a d"),
                                in_=t_o.rearrange("p (a d) -> p a d", a=4)))
                        else:
                            sb0 = 2 + tok0 // 128
                            op(DVE, [rpt_], r_Vb[sb0:sb0 + 4], lambda: nc.vector.tensor_copy(
                                out=Vb[:, sb0:sb0 + 4, kv, :], in_=pt_[:].rearrange("p (a d) -> p a d", a=4)))
                    elif mc < 24:
                        pend[mc] = (p, rp)
                    else:
                        pu, rpu = pend.pop(mc - 4)
                        ch = mc - 24
                        t_e, r_e = ntf()
                        op(ACT, [rp], [r_e], lambda: nc.scalar.activation(out=t_e, in_=p[:], func=AF.Exp, scale=-1.0))
                        op(DVE, [r_e], [r_e], lambda: nc.vector.tensor_scalar(out=t_e, in0=t_e, scalar1=1.0, scalar2=0.5, op0=ALU.add, op1=ALU.mult))
                        op(DVE, [r_e], [r_e], lambda: nc.vector.reciprocal(out=t_e, in_=t_e))
                        t_o, r_o = ntb()
                        op(DVE, [r_e, rpu], [r_o], lambda: nc.vector.tensor_tensor(out=t_o, in0=pu[:], in1=t_e, op=ALU.mult))
                        if s == 0:
                            dma(QS, [r_o], [r_uT[b]], lambda: nc.sync.dma_start(out=uT_d[ch, :, 16 + b * T:16 + (b + 1) * T], in_=t_o))
                        else:
                            dma(QS, [r_o], [r_uTP], lambda: nc.sync.dma_start(
                                out=uT_dP[ch, :, :].rearrange("p (s w) -> p s w", s=2)[:, :, 16:272],
                                in_=t_o.rearrange("p (s w) -> p s w", s=2)))

            ms("s1p" if s == 1 else "s1s")
            def attn_loads(b):
                tok0 = b * T if s == 0 else 0
                for grp in range(3):
                    dma(QL, [r_qT[grp][b]], [r_q[grp]], lambda: nc.sync.dma_start(
                        out=qblk[:, grp * 4:(grp + 1) * 4, :], in_=qT_d[grp * 4:(grp + 1) * 4, :, tok0:tok0 + T].rearrange("h p t -> p h t")))
                if s == 0:
                    nb_ = [r_uT[x] for x in (b - 1, b, b + 1) if 0 <= x < NSB]
                    dma(QL, nb_, [r_uh], lambda: nc.sync.dma_start(out=uh[:, :, 0:542], in_=uT_d[:, :, b * T + 1:b * T + 543].rearrange("c p w -> p c w")))
                    nbk_ = [r_Ka_d[x] for x in (b - 1, b, b + 1) if 0 <= x < NSB]
                    dma(QL, nbk_, [r_KaW], lambda: nc.sync.dma_start(out=KaW[:], in_=KaT_d[:, :, tok0:tok0 + 768].rearrange("k p t -> p k t")))
                    nbv_ = [r_Va_d[x] for x in (b - 1, b, b + 1) if 0 <= x < NSB]
                    dma(QL, nbv_, [r_VaW], lambda: nc.sync.dma_start(out=VaW[:], in_=Va_d[tok0 // 128:tok0 // 128 + 6].rearrange("a p d -> p a d")))
                else:
                    dma(QL, [r_uTP], [r_uh], lambda: nc.sync.dma_start(out=uh[:, :, :], in_=uT_dP[:, :, :].rearrange("c p w -> p c w")))
                    dma(QL, [r_Ka_d[0]], [r_KaW], lambda: nc.sync.dma_start(out=KaW[:, :, 0:640], in_=KaT_d[:, :, 0:640].rearrange("k p t -> p k t")))
                    dma(QL, [r_Va_d[0]], [r_VaW], lambda: nc.sync.dma_start(out=VaW[:, 0:5], in_=Va_d[0:5].rearrange("a p d -> p a d")))


            for b in range(nblk):
                tok0 = b * T if s == 0 else 0
                if b == 0:
                    attn_loads(0)

                conv_ops = []
                nseq = 1 if s == 0 else 2
                wdt = T // nseq
                for sq in range(nseq):
                    base = sq * 288 + 1 if s == 1 else 0
                    for k in range(31):
                        for ch in range(4):
                            def _cop(sq=sq, base=base, k=k, ch=ch):
                                src = uh[:, ch, base + k:base + k + wdt]
                                acc = cacc[:, ch, sq * wdt:(sq + 1) * wdt]
                                if k == 0:
                                    op(DVE, [r_uh, r_k, r_vec], [r_cacc[ch]], lambda: nc.vector.tensor_scalar(
                                        out=acc, in0=src, scalar1=cwh[:, l, ch * 31:ch * 31 + 1], scalar2=V(l, OFF_CB + ch), op0=ALU.mult, op1=ALU.add))
                                else:
                                    op(DVE, [r_uh, r_k, r_cacc[ch]], [r_cacc[ch]], lambda: nc.vector.scalar_tensor_tensor(
                                        out=acc, in0=src, scalar=cwh[:, l, ch * 31 + k:ch * 31 + k + 1], in1=acc, op0=ALU.mult, op1=ALU.add))
                            conv_ops.append(_cop)

                def conv_step(n):
                    for _ in range(n * nseq):
                        if conv_ops:
                            conv_ops.pop(0)()

                def attn_unit(qap, rq, nh, segs, out_ap, r_out, sink_cols):
                    n = nh * 128
                    po, rpo = ps[3 + ctr["pt"] % 2], r_ps[3 + ctr["pt"] % 2]
                    pd, rpd = ps[5 + ctr["pt"] % 2], r_ps[5 + ctr["pt"] % 2]
                    ctr["pt"] += 1
                    go = PEGroup(PE, [rpo])
                    gd = PEGroup(PE, [rpd])
                    prev = None
                    for i in range(len(segs) + 1):
                        if i < len(segs):
                            kT, rk, v, rv, m = segs[i]
                            j = ctr["main"] % 3
                            ctr["main"] += 1
                            pS, rpS = ps[j], r_ps[j]
                            g = PEGroup(PE, [rpS])
                            g.mm([rk, rq], lambda: nc.tensor.matmul(pS[:, 0:n].rearrange("p (h q) -> p h q", h=nh), lhsT=kT, rhs=qap, start=True, stop=(m is None)))
                            if m is not None:
                                g.mm([r_k], lambda: nc.tensor.matmul(pS[:, 0:n], lhsT=ident_bf[:], rhs=m, start=False, stop=True))
                            g.done()
                            jp = ctr["pring"] % NPT
                            ctr["pring"] += 1
                            op(ACT, [rpS], [r_PT[jp]], lambda: nc.scalar.activation(out=PT[:, jp, 0:n], in_=pS[:, 0:n], func=AF.Exp, scale=SCALE))
                            cur = (jp, v, rv)
                        else:
                            cur = None
                        if prev is not None:
                            jp0, v0, rv0 = prev
                            first = (i == 1)
                            lastseg = (i == len(segs))
                            go.mm([r_PT[jp0], rv0], lambda: nc.tensor.matmul(po[:, 0:n], lhsT=v0, rhs=PT[:, jp0, 0:n], start=first, stop=lastseg))
                            gd.mm([r_PT[jp0], r_k], lambda: nc.tensor.matmul(pd[:, 0:n], lhsT=ones_bf[:], rhs=PT[:, jp0, 0:n], start=first, stop=lastseg))
                        prev = cur
                    go.done()
                    gd.done()
                    t_r, r_r = ntf()
                    if sink_cols is not None:
                        for h in range(nh):
                            op(DVE, [rpd, r_k], [r_r], lambda: nc.vector.tensor_scalar(
                                out=t_r[:, h * 128:(h + 1) * 128], in0=pd[:, h * 128:(h + 1) * 128], scalar1=esink[:, sink_cols + h:sink_cols + h + 1], scalar2=None, op0=ALU.add))
                        op(DVE, [r_r], [r_r], lambda: nc.vector.reciprocal(out=t_r[:, 0:n], in_=t_r[:, 0:n]))
                    else:
                        op(DVE, [rpd], [r_r], lambda: nc.vector.reciprocal(out=t_r[:, 0:n], in_=pd[:, 0:n]))
                    op(DVE, [rpo, r_r], r_out, lambda: nc.vector.tensor_tensor(
                        out=out_ap, in0=po[:, 0:n].rearrange("p (h q) -> p h q", h=nh), in1=t_r[:, 0:n].rearrange("p (h q) -> p h q", h=nh), op=ALU.mult))

                for kv in range(2):
                    for qs in range(4):
                        qap = qblk[:, kv * 2:kv * 2 + 2, qs * 128:(qs + 1) * 128]
                        segs = []
                        if s == 0:
                            for st in range(2):
                                segs.append((KaC[:, kv, st * 128:(st + 1) * 128], r_KaC, VaC[:, st, kv, :], r_VaC, None))
                            gq = b * 4 + qs
                            for dlt, m in ((-1, mask_bf[:, 0, :]), (0, None), (1, mask_bf[:, 1, :])):
                                if 0 <= gq + dlt < 32:
                                    w0 = (qs + 1 + dlt) * 128
                                    segs.append((KaW[:, kv, w0:w0 + 128], r_KaW, VaW[:, qs + 1 + dlt, kv * 128:(kv + 1) * 128], r_VaW, m))
                        else:
                            sq = qs // 2
                            for st in range(2):
                                w0 = 128 + sq * 256 + st * 128
                                segs.append((KaW[:, kv, w0:w0 + 128], r_KaW, VaW[:, 1 + sq * 2 + st, kv * 128:(kv + 1) * 128], r_VaW, None))
                        attn_unit(qap, r_q[0], 2, segs, actb[:, kv * 2:kv * 2 + 2, qs * 128:(qs + 1) * 128], r_act[kv * 2:kv * 2 + 2], l * 4 + kv * 2)
                        conv_step(4)
                for kv in range(2):
                    for qs in range(4):
                        qap = qblk[:, 4 + kv * 4:8 + kv * 4, qs * 128:(qs + 1) * 128]
                        if s == 0:
                            sbl = list(range(34))
                        else:
                            sq = qs // 2
                            sbl = [2 + sq * 2, 3 + sq * 2]
                        segs = [(KTb[:, kv, j * 128:(j + 1) * 128], r_KTb[j], Vb[:, j, kv, :], r_Vb[j], None) for j in sbl]
                        attn_unit(qap, r_q[1 + kv], 4, segs, actb[:, 4 + kv * 4:8 + kv * 4, qs * 128:(qs + 1) * 128], r_act[4 + kv * 4:8 + kv * 4], None)
                        conv_step(12)

                ms("attn")
                conv_step(10 ** 6)
                psum_, rps_ = pss()
                psq_, rpq_ = pss()
                g1_ = PEGroup(PE, [rps_])
                g2_ = PEGroup(PE, [rpq_])
                for ch in range(4):
                    t1, rt1 = ntb()
                    op(ACT, [r_cacc[ch]], [rt1], lambda: nc.scalar.copy(out=t1, in_=cacc[:, ch, :]))
                    g1_.mm([rt1, r_k], lambda: nc.tensor.matmul(psum_[:], lhsT=ones_bf[:], rhs=t1, start=(ch == 0), stop=(ch == 3)), sig=True)
                    t2, rt2 = ntb()
                    op(ACT, [r_cacc[ch]], [rt2], lambda: nc.scalar.activation(out=t2, in_=cacc[:, ch, :], func=AF.Square))
                    g2_.mm([rt2, r_k], lambda: nc.tensor.matmul(psq_[:], lhsT=ones_bf[:], rhs=t2, start=(ch == 0), stop=(ch == 3)), sig=True)
                g1_.done()
                g2_.done()
                t_m, r_m = rstd[:], r_rstd
                op(DVE, [rps_], [r_m], lambda: nc.vector.tensor_scalar(out=t_m, in0=psum_[:], scalar1=1.0 / 512, scalar2=None, op0=ALU.mult))
                t_v, r_v = ropeCt[:], r_rope
                op(DVE, [r_m], [r_v], lambda: nc.vector.tensor_tensor(out=t_v, in0=t_m, in1=t_m, op=ALU.mult))
                op(DVE, [rpq_, r_v], [r_v], lambda: nc.vector.scalar_tensor_tensor(out=t_v, in0=psq_[:], scalar=1.0 / 512, in1=t_v, op0=ALU.mult, op1=ALU.subtract))
                op(ACT, [r_v, r_k], [r_v], lambda: nc.scalar.activation(out=t_v, in_=t_v, func=AF.Ln, bias=eps_t[:], scale=1.0))
                op(ACT, [r_v], [r_v], lambda: nc.scalar.activation(out=t_v, in_=t_v, func=AF.Exp, scale=-0.5))
                for ch in range(4):
                    op(DVE, [r_cacc[ch], r_m], [r_cacc[ch]], lambda: nc.vector.tensor_tensor(out=cacc[:, ch, :], in0=cacc[:, ch, :], in1=t_m, op=ALU.subtract))
                for ch in range(4):
                    op(DVE, [r_cacc[ch], r_v], [r_cacc[ch]], lambda: nc.vector.tensor_tensor(out=cacc[:, ch, :], in0=cacc[:, ch, :], in1=t_v, op=ALU.mult))
                for ch in range(4):
                    op(ACT, [r_cacc[ch], r_vec], [r_cacc[ch]], lambda: nc.scalar.activation(out=cacc[:, ch, :], in_=cacc[:, ch, :], func=AF.Identity, bias=V(l, OFF_LB + ch), scale=V(l, OFF_LG + ch)))
                tl = []
                for ch in range(4):
                    t2, rt2 = ntf()
                    tl.append((t2, rt2))
                    op(ACT, [r_cacc[ch]], [rt2], lambda: nc.scalar.activation(out=t2, in_=cacc[:, ch, :], func=AF.Exp, scale=-1.0))
                for ch in range(4):
                    t2, rt2 = tl[ch]
                    op(DVE, [rt2], [rt2], lambda: nc.vector.tensor_scalar(out=t2, in0=t2, scalar1=1.0, scalar2=None, op0=ALU.add))
                for ch in range(4):
                    t2, rt2 = tl[ch]
                    op(DVE, [rt2], [rt2], lambda: nc.vector.reciprocal(out=t2, in_=t2))
                for ch in range(4):
                    t2, rt2 = tl[ch]
                    op(DVE, [r_cacc[ch], rt2], [r_act[12 + ch]], lambda: nc.vector.tensor_tensor(out=actb[:, 12 + ch, :], in0=cacc[:, ch, :], in1=t2, op=ALU.mult))

                ms("conv")
                load_x(s, b)
                for mc in range(16):
                    wt, rw = wnext(None)
                    p, rp = pmain()
                    big_mm(wt, rw, KC, lambda k: actb[:, k, :], r_act, p, rp)
                    op(DVE, [rp, r_x[mc], r_mod[l]], [r_x[mc]], lambda: nc.vector.scalar_tensor_tensor(
                        out=xblk[:, mc, :], in0=p[:], scalar=modv[:, l, s, 2, mc:mc + 1], in1=xblk[:, mc, :], op0=ALU.mult, op1=ALU.add))

                ms("wout")
                norm_modulate(l, s, 1)
                if b + 1 < nblk:
                    attn_loads(b + 1)
                for hf in range(4):
                    for fc in range(11):
                        wg_, rwg = wnext(None)
                        pg, rpg = pmain()
                        big_mm(wg_, rwg, KC, lambda k: actb[:, k, :], r_act, pg, rpg)
                        wu_, rwu = wnext(None)
                        pu, rpu = pmain()
                        big_mm(wu_, rwu, KC, lambda k: actb[:, k, :], r_act, pu, rpu)
                        t1, rt1 = ntf()
                        op(ACT, [rpg], [rt1], lambda: nc.scalar.activation(out=t1, in_=pg[:], func=AF.Silu))
                        op(DVE, [rt1, rpu], [r_g[fc]], lambda: nc.vector.tensor_tensor(out=gT[:, fc, :], in0=pu[:], in1=t1, op=ALU.mult))
                    for mc in range(16):
                        wt, rw = wnext(None)
                        p, rp = pmain()
                        big_mm(wt, rw, 11, lambda k: gT[:, k, :], r_g, p, rp)
                        op(DVE, [rp, r_x[mc], r_mod[l]], [r_x[mc]], lambda: nc.vector.scalar_tensor_tensor(
                            out=xblk[:, mc, :], in0=p[:], scalar=modv[:, l, s, 5, mc:mc + 1], in1=xblk[:, mc, :], op0=ALU.mult, op1=ALU.add))

                ms("ffn")
                if not last:
                    xT = xT_S if s == 0 else xT_P
                    rr = r_xT_S[b] if s == 0 else r_xT_P[0]
                    dma(QS, r_x, [rr], lambda: nc.sync.dma_start(out=xT[:, :, b * T:(b + 1) * T].rearrange("k p t -> p k t"), in_=xblk[:]))
                else:
                    ydst = y_s if s == 0 else y_p
                    for a in range(4):
                        for q4 in range(4):
                            pt_, rpt_ = paux()
                            g = PEGroup(PE, [rpt_])
                            for j in range(4):
                                kc = q4 * 4 + j
                                g.mm([r_x[kc], r_cst], lambda: nc.tensor.transpose(pt_[:, j * 128:(j + 1) * 128], xblk[:, kc, a * 128:(a + 1) * 128], ident_f))
                            g.done()
                            t1, rt1 = ntf()
                            if q4 % 2 == 0:
                                op(DVE, [rpt_], [rt1], lambda: nc.vector.tensor_copy(out=t1, in_=pt_[:]))
                            else:
                                op(ACT, [rpt_], [rt1], lambda: nc.scalar.copy(out=t1, in_=pt_[:]))
                            dma(QS, [rt1], [], lambda: nc.sync.dma_start(out=ydst[tok0 + a * 128:tok0 + (a + 1) * 128, q4 * 512:(q4 + 1) * 512], in_=t1))

    return nc


def _fm(v):
    return np.ascontiguousarray(v.reshape(-1, 128).T)


def _consts():
    c = np.zeros((128, 768), np.float32)
    c[:, 0:128] = np.eye(128, dtype=np.float32)
    k = np.arange(128)
    c[(k + 64) % 128, 128 + k] = 1.0
    j = np.arange(128)[:, None]
    i = np.arange(128)[None, :]
    lo = np.where(j >= i, 0.0, NEG).astype(np.float32)
    hi = np.where(j <= i, 0.0, NEG).astype(np.float32)
    c[:, 256:384] = lo
    c[:, 384:512] = lo
    c[:, 512:640] = hi
    c[:, 640:768] = hi
    return c


def _rope():
    t = np.arange(SEQ_S)
    row = (t // 64).astype(np.float32)
    col = (t % 64).astype(np.float32)
    inv = np.power(np.float32(10000.0), -np.arange(32, dtype=np.float32) / np.float32(32)).astype(np.float32)
    ang = np.concatenate([row[:, None] * inv, col[:, None] * inv], axis=-1).astype(np.float32)
    cos = np.cos(ang).astype(np.float32).T
    sin = np.sin(ang).astype(np.float32).T
    C = np.concatenate([cos, cos], axis=0)
    S = np.concatenate([-sin, sin], axis=0)
    return np.ascontiguousarray(C), np.ascontiguousarray(S)


_CACHE = {}


def kernel(x_prompt, x_sample, cache_a_k, cache_a_v, cache_b_k, cache_b_v, c, c_ctx,
           w_ada, b_ada, w_in, w_out, w_gate, w_up, w_down, norm1_g, norm2_g,
           qnorm_a_g, knorm_a_g, qnorm_b_g, knorm_b_g, sink_a,
           conv_w, conv_b, conv_ln_g, conv_ln_b, _depth=None, _dbg=None):
    f = lambda a: np.ascontiguousarray(np.asarray(a, dtype=np.float32))
    L = int(_depth) if _depth else DEPTH
    n = 8
    key = L
    if key not in _CACHE:
        _CACHE[key] = build(L, _dbg)
    nc = _CACHE[key]
    x_prompt, x_sample = f(x_prompt), f(x_sample)
    cak, cav, cbk, cbv = f(cache_a_k), f(cache_a_v), f(cache_b_k), f(cache_b_v)
    c, c_ctx = f(c), f(c_ctx)
    shared = dict(w_ada=f(w_ada)[:L], w_in=f(w_in)[:L], w_out=f(w_out)[:L], w_gate=f(w_gate)[:L], w_up=f(w_up)[:L], w_down=f(w_down)[:L],
                  consts=_consts())
    shared["ropeC"], shared["ropeS"] = _rope()
    base = np.zeros((128, 32 + L * LV), np.float32)
    base[:, 16:32] = _fm(c_ctx)
    b_ada, norm1_g, norm2_g = f(b_ada), f(norm1_g), f(norm2_g)
    qa, ka, qb, kb = f(qnorm_a_g), f(knorm_a_g), f(qnorm_b_g), f(knorm_b_g)
    sink_a, conv_w, conv_b, lg, lb = f(sink_a), f(conv_w), f(conv_b), f(conv_ln_g), f(conv_ln_b)
    for l in range(L):
        o = 32 + l * LV
        base[:, o:o + 16] = _fm(norm1_g[l])
        base[:, o + 16:o + 32] = _fm(norm2_g[l])
        base[:, o + 32:o + 128] = _fm(b_ada[l])
        base[:, o + 128] = qa[l]
        base[:, o + 129] = ka[l]
        base[:, o + 130] = qb[l]
        base[:, o + 131] = kb[l]
        base[:, o + 132:o + 256] = conv_w[l].T.reshape(4, 128, 31).transpose(1, 0, 2).reshape(128, 124)
        base[:, o + 256:o + 260] = _fm(conv_b[l])
        base[:, o + 260:o + 264] = _fm(lg[l])
        base[:, o + 264:o + 268] = _fm(lb[l])
        base[:, o + 268:o + 272] = np.broadcast_to(sink_a[l][None, :], (128, 4))
    in_maps = []
    for i in range(n):
        v = base.copy()
        v[:, 0:16] = _fm(c[i])
        m = dict(shared)
        m.update(x_s=x_sample[i], x_p=np.ascontiguousarray(x_prompt[2 * i:2 * i + 2].reshape(T, D)),
                 cak=np.ascontiguousarray(cak[i, :L]), cav=np.ascontiguousarray(cav[i, :L]),
                 cbk=np.ascontiguousarray(cbk[i, :L]), cbv=np.ascontiguousarray(cbv[i, :L]), vecs=v)
        in_maps.append(m)
    res = run_bass_kernel_spmd(nc, in_maps, core_ids=list(range(n)))
    R = res.results
    y_prompt = np.concatenate([r["y_p"].reshape(2, 256, D) for r in R], axis=0)
    y_sample = np.stack([r["y_s"] for r in R], axis=0)
    outs = [np.concatenate([r[k] for r in R], axis=0) for k in ("nak", "nav", "nbk", "nbv")]
    return (y_prompt, y_sample, outs[0], outs[1], outs[2], outs[3])
```

```python
import numpy as np
import concourse.bass as bass
import concourse.mybir as mybir
from concourse.bass_utils import run_bass_kernel_spmd

F32 = mybir.dt.float32
BF16 = mybir.dt.bfloat16
AF = mybir.ActivationFunctionType
ALU = mybir.AluOpType

D = 2048
KC = 16
DFF = 5632
FC = 44
T = 512
DEPTH = 4
NSB = 8
SEQ_S = 4096
EPS = 1e-6
SCALE = 128 ** -0.5
LV = 272
NEG = -30000.0


class Ev:
    __slots__ = ("sem", "val", "clock")

    def __init__(self, sem, val, clock):
        self.sem = sem
        self.val = val
        self.clock = clock


class Res:
    __slots__ = ("w", "r", "ex")

    def __init__(self, ex=False):
        self.w = None
        self.r = {}
        self.ex = ex


def RL(n):
    return [Res() for _ in range(n)]


class Eng:
    def __init__(self, nc, e, name, self_ordered=False):
        self.e = e
        self.sem = nc.alloc_semaphore("s_" + name)
        self.key = "E" + name
        self.cnt = 0
        self.seen = {}
        self.self_ordered = self_ordered

    def wait(self, ev):
        if ev is None or self.seen.get(ev.sem[0], 0) >= ev.val:
            return
        self.e.wait_ge(ev.sem[1], ev.val)
        for s, v in ev.clock.items():
            if self.seen.get(s, 0) < v:
                self.seen[s] = v
        self.seen[ev.sem[0]] = ev.val

    def signal(self, inst):
        self.cnt += 1
        inst.then_inc(self.sem, 1)
        if self.self_ordered:
            self.seen[self.key] = self.cnt
        return Ev((self.key, self.sem), self.cnt, dict(self.seen))


class DmaQ:
    def __init__(self, nc, eng, name, nsem):
        self.E = eng
        self.sems = [[(name + str(i), nc.alloc_semaphore("d_" + name + str(i))), 0, None] for i in range(nsem)]
        self.i = 0


def _deps(reads, writes):
    out = []
    for x in reads:
        if x.w is not None:
            out.append(x.w)
    for x in writes:
        if x.w is not None:
            out.append(x.w)
        out.extend(x.r.values())
    return out


def _commit(ev, reads, writes):
    k = ev.sem[0]
    for x in reads:
        c = x.r.get(k)
        if c is None or c.val < ev.val:
            x.r[k] = ev
    for x in writes:
        x.w = ev
        x.r = {}


def op(E, reads, writes, fn):
    if any(x.ex for x in reads):
        writes = list(writes) + [x for x in reads if x.ex]
        reads = [x for x in reads if not x.ex]
    for ev in _deps(reads, writes):
        E.wait(ev)
    ev = E.signal(fn())
    _commit(ev, reads, writes)
    return ev


def dma(Q, reads, writes, fn):
    E = Q.E
    for ev in _deps(reads, writes):
        E.wait(ev)
    s = Q.sems[Q.i]
    Q.i = (Q.i + 1) % len(Q.sems)
    E.wait(s[2])
    s[1] += 16
    fn().then_inc(s[0][1], 16)
    ev = Ev(s[0], s[1], dict(E.seen))
    s[2] = ev
    _commit(ev, reads, writes)
    return ev


class PEGroup:
    def __init__(self, PE, writes):
        self.PE = PE
        self.writes = writes
        self.reads = []
        self.last = None
        for ev in _deps([], writes):
            PE.wait(ev)

    def mm(self, reads, fn, sig=False):
        for x in reads:
            if x.w is not None:
                self.PE.wait(x.w)
        self.last = fn()
        if sig:
            ev = self.PE.signal(self.last)
            _commit(ev, reads, [])
            self.sig_last = True
        else:
            self.reads.extend(reads)
            self.sig_last = False

    def done(self):
        if getattr(self, "sig_last", False):
            ev = Ev((self.PE.key, self.PE.sem), self.PE.cnt, dict(self.PE.seen))
        else:
            ev = self.PE.signal(self.last)
        _commit(ev, self.reads, self.writes)
        return ev


class _Stop(Exception):
    pass


def build(depth=DEPTH, dbg=None):
    nc = bass.Bass("TRN2", target_bir_lowering=False)
    L = depth
    fin = {}
    try:
        _build_body(nc, L, dbg, fin)
    except _Stop:
        pass
    POOL, QS, QL, engs = fin["POOL"], fin["QS"], fin["QL"], fin["engs"]
    for Q in (QS, QL, fin["QP"]):
        for s_ in Q.sems:
            POOL.wait(s_[2])
    for E in engs:
        if E.cnt:
            POOL.e.wait_ge(E.sem, E.cnt)
    return nc


def _build_body(nc, L, dbg, fin):
    def ms(name):
        if dbg == name:
            raise _Stop()

    def din(name, shape, dt=F32):
        return nc.dram_tensor(name, list(shape), dt, kind="ExternalInput").ap()

    def dout(name, shape):
        return nc.dram_tensor(name, list(shape), F32, kind="ExternalOutput").ap()

    def dscr(name, shape, dt):
        return nc.dram_tensor(name, list(shape), dt, kind="Internal").ap()

    x_s = din("x_s", [SEQ_S, D])
    x_p = din("x_p", [T, D])
    cak = din("cak", [L, 256, 2, 128])
    cav = din("cav", [L, 256, 2, 128])
    cbk = din("cbk", [L, 256, 2, 128])
    cbv = din("cbv", [L, 256, 2, 128])
    vecs = din("vecs", [128, 32 + L * LV])
    consts = din("consts", [128, 128 * 2 + 512])
    ropeC = din("ropeC", [128, SEQ_S])
    ropeS = din("ropeS", [128, SEQ_S])
    w_ada = din("w_ada", [L, D, 6 * D])
    w_in = din("w_in", [L, D, 3584])
    w_out = din("w_out", [L, D, D])
    w_gate = din("w_gate", [L, D, DFF])
    w_up = din("w_up", [L, D, DFF])
    w_down = din("w_down", [L, DFF, D])
    y_s = dout("y_s", [SEQ_S, D])
    y_p = dout("y_p", [T, D])
    nak = dout("nak", [2, L, 256, 2, 128])
    nav = dout("nav", [2, L, 256, 2, 128])
    nbk = dout("nbk", [2, L, 256, 2, 128])
    nbv = dout("nbv", [2, L, 256, 2, 128])

    xT_S = dscr("xT_S", [KC, 128, SEQ_S], F32)
    xT_P = dscr("xT_P", [KC, 128, T], F32)
    Wt_in = [dscr(f"Wt_in{l}", [28, 128, 2048], BF16) for l in range(L)]
    Wt_out = [dscr(f"Wt_out{l}", [16, 128, 2048], BF16) for l in range(L)]
    Wt_g = [dscr(f"Wt_g{l}", [FC, 128, 2048], BF16) for l in range(L)]
    Wt_u = [dscr(f"Wt_u{l}", [FC, 128, 2048], BF16) for l in range(L)]
    Wt_d = [dscr(f"Wt_d{l}", [16, 4, 128, 11 * 128], BF16) for l in range(L)]
    qT_d = dscr("qT_d", [12, 128, SEQ_S], BF16)
    uT_d = dscr("uT_d", [4, 128, SEQ_S + 32], BF16)
    uT_dP = dscr("uT_dP", [4, 128, 2 * 288], BF16)
    KaT_d = dscr("KaT_d", [2, 128, SEQ_S + 256], BF16)
    Va_d = dscr("Va_d", [34, 128, 256], BF16)

    PE = Eng(nc, nc.tensor, "pe", self_ordered=True)
    ACT = Eng(nc, nc.scalar, "act")
    DVE = Eng(nc, nc.vector, "dve")
    POOL = Eng(nc, nc.gpsimd, "pool")
    SP = Eng(nc, nc.sync, "sp")
    QL = DmaQ(nc, SP, "ql", 40)
    QS = DmaQ(nc, SP, "qs", 24)
    QP = DmaQ(nc, POOL, "qp", 4)
    fin.update(POOL=POOL, QS=QS, QL=QL, QP=QP, engs=(PE, ACT, DVE))

    def sb(name, shape, dt):
        return nc.alloc_sbuf_tensor(name, list(shape), dt)

    vec_t = sb("vec_t", [128, 32 + L * LV], F32)
    r_vec = Res()
    cst_t = sb("cst_t", [128, 768], F32)
    r_cst = Res()
    ident_bf = sb("ident_bf", [128, 128], BF16)
    ones_bf = sb("ones_bf", [128, 128], BF16)
    mask_bf = sb("mask_bf", [128, 2, 256], BF16)
    eps_t = sb("eps_t", [128, 1], F32)
    r_k = Res()
    modv = sb("modv", [128, L, 2, 6, KC], F32)
    r_mod = RL(L)
    moda = sb("moda", [128, L, 2, 2, KC], F32)
    esink = sb("esink", [128, L * 4], F32)
    cwh = sb("cwh", [128, L, 124], F32)
    ident_f = cst_t[:, 0:128]
    perm_f = cst_t[:, 128:256]

    def V(l, off, n=1):
        b = 32 + l * LV + off
        return vec_t[:, b:b + n]

    OFF_G1, OFF_G2, OFF_BADA, OFF_QGA, OFF_KGA, OFF_QGB, OFF_KGB = 0, 16, 32, 128, 129, 130, 131
    OFF_CW, OFF_CB, OFF_LG, OFF_LB, OFF_SINK = 132, 256, 260, 264, 268

    ps = [nc.alloc_psum_tensor(f"ps{i}", [128, T], F32) for i in range(8)]
    r_ps = [Res(ex=True) for _ in range(8)]

    dma(QL, [], [r_vec], lambda: nc.sync.dma_start(out=vec_t[:], in_=vecs[:, :]))
    dma(QL, [], [r_cst], lambda: nc.sync.dma_start(out=cst_t[:], in_=consts[:, :]))
    op(DVE, [r_cst], [r_k], lambda: nc.vector.tensor_copy(out=ident_bf[:], in_=ident_f))
    op(DVE, [r_cst], [r_k], lambda: nc.vector.tensor_copy(out=mask_bf[:].rearrange("p a b -> p (a b)"), in_=cst_t[:, 256:768]))
    op(DVE, [], [r_k], lambda: nc.vector.memset(ones_bf[:], 1.0))
    op(DVE, [], [r_k], lambda: nc.vector.memset(eps_t[:], EPS))
    for l in range(L):
        op(ACT, [r_vec], [r_k], lambda: nc.scalar.activation(out=esink[:, l * 4:l * 4 + 4], in_=V(l, OFF_SINK, 4), func=AF.Exp))
        op(DVE, [r_vec], [r_k], lambda: nc.vector.tensor_scalar(out=cwh[:, l, :], in0=V(l, OFF_CW, 124), scalar1=0.5, scalar2=None, op0=ALU.mult))

    r_xT_S = RL(NSB)
    r_xT_P = RL(1)
    r_W = [dict(inn=Res(), out=Res(), g=Res(), u=Res(), d=Res()) for _ in range(L)]

    with nc.sbuf_tensor("p0_x", [128, 4, D], F32) as p0_x, \
            nc.sbuf_tensor("p0_st", [128, KC, T], F32) as p0_st, \
            nc.sbuf_tensor("p0_wa", [128, 2, KC, 512], F32) as p0_wa, \
            nc.sbuf_tensor("p0_wb", [128, 2, 4 * KC * 128], BF16) as p0_wb, \
            nc.sbuf_tensor("p0_sc", [128, KC, 2], F32) as p0_sc, \
            nc.sbuf_tensor("p0_zb", [128, 512], BF16) as p0_zb, \
            nc.sbuf_tensor("adat", [2, 512], F32) as adat:
        r_adat = Res()
        r_p0x, r_p0st, r_wa, r_wb, r_sc, r_zb = Res(), RL(KC), RL(2), RL(2), Res(), Res()

        op(DVE, [], [r_zb], lambda: nc.vector.memset(p0_zb[:], 0.0))
        for c in range(4):
            dma(QS, [r_zb], [], lambda: nc.sync.dma_start(out=uT_d[c, :, 0:16], in_=p0_zb[:, 0:16]))
            dma(QS, [r_zb], [], lambda: nc.sync.dma_start(out=uT_d[c, :, SEQ_S + 16:SEQ_S + 32], in_=p0_zb[:, 0:16]))
            dma(QS, [r_zb], [], lambda: nc.sync.dma_start(out=uT_dP[c, :, :].rearrange("p (s w) -> p s w", s=2)[:, :, 0:16], in_=p0_zb[:, 0:32].rearrange("p (s w) -> p s w", s=2)))
            dma(QS, [r_zb], [], lambda: nc.sync.dma_start(out=uT_dP[c, :, :].rearrange("p (s w) -> p s w", s=2)[:, :, 272:288], in_=p0_zb[:, 0:32].rearrange("p (s w) -> p s w", s=2)))
        for kv in range(2):
            dma(QS, [r_zb], [], lambda: nc.sync.dma_start(out=KaT_d[kv, :, 0:128], in_=p0_zb[:, 0:128]))
            dma(QS, [r_zb], [], lambda: nc.sync.dma_start(out=KaT_d[kv, :, SEQ_S + 128:SEQ_S + 256], in_=p0_zb[:, 0:128]))
        dma(QS, [r_zb], [], lambda: nc.sync.dma_start(out=Va_d[0], in_=p0_zb[:, 0:256]))
        dma(QS, [r_zb], [], lambda: nc.sync.dma_start(out=Va_d[33], in_=p0_zb[:, 0:256]))

        ms("p0a")
        def to_feature_major(xsrc, xT, r_dst, t0):
            dma(QL, [], [r_p0x], lambda: nc.sync.dma_start(out=p0_x[:], in_=xsrc[t0:t0 + T, :].rearrange("(a p) f -> p a f", p=128)))
            for kc in range(KC):
                b = kc % 2
                g = PEGroup(PE, [r_ps[b]])
                for a in range(4):
                    g.mm([r_p0x, r_cst], lambda: nc.tensor.transpose(ps[b][:, a * 128:(a + 1) * 128], p0_x[:, a, kc * 128:(kc + 1) * 128], ident_f))
                g.done()
                if kc % 2 == 0:
                    op(DVE, [r_ps[b]], [r_p0st[kc]], lambda: nc.vector.tensor_copy(out=p0_st[:, kc, :], in_=ps[b][:]))
                else:
                    op(ACT, [r_ps[b]], [r_p0st[kc]], lambda: nc.scalar.copy(out=p0_st[:, kc, :], in_=ps[b][:]))
            dma(QS, r_p0st, [r_dst], lambda: nc.sync.dma_start(out=xT[:, :, t0:t0 + T].rearrange("k p t -> p k t"), in_=p0_st[:]))

        to_feature_major(x_p, xT_P, r_xT_P[0], 0)
        for b in range(NSB):
            to_feature_major(x_s, xT_S, r_xT_S[b], b * T)

        ms("p0b")
        op(ACT, [r_vec], [r_sc], lambda: nc.scalar.activation(out=p0_sc[:, :, 0], in_=vec_t[:, 0:16], func=AF.Silu))
        op(ACT, [r_vec], [r_sc], lambda: nc.scalar.activation(out=p0_sc[:, :, 1], in_=vec_t[:, 16:32], func=AF.Silu))
        cnt = 0
        for l in range(L):
            for q in range(24):
                sl = cnt % 2
                cnt += 1
                dma(QL, [], [r_wa[sl]], lambda: nc.sync.dma_start(out=p0_wa[:, sl], in_=w_ada[l, :, q * 512:(q + 1) * 512].rearrange("(k p) n -> p k n", p=128)))
                g = PEGroup(PE, [r_ps[2]])
                for kc in range(KC):
                    g.mm([r_wa[sl], r_sc], lambda: nc.tensor.matmul(ps[2][0:2, :], lhsT=p0_sc[:, kc, :], rhs=p0_wa[:, sl, kc, :], start=(kc == 0), stop=(kc == KC - 1)))
                g.done()
                op(ACT, [r_ps[2]], [r_adat], lambda: nc.scalar.copy(out=adat[:], in_=ps[2][0:2, :]))
                g = PEGroup(PE, [r_ps[3]])
                for j in range(4):
                    g.mm([r_adat, r_cst], lambda: nc.tensor.transpose(ps[3][:, j * 2:j * 2 + 2], adat[0:2, j * 128:(j + 1) * 128], cst_t[0:2, 0:2]))
                g.done()
                i0, k0 = (q * 4) // 16, (q * 4) % 16
                for s in range(2):
                    op(DVE, [r_ps[3], r_vec], [r_mod[l]], lambda: nc.vector.tensor_tensor(
                        out=modv[:, l, s, i0, k0:k0 + 4], in0=ps[3][:, 0:8].rearrange("p (j s) -> p j s", s=2)[:, :, s],
                        in1=V(l, OFF_BADA + q * 4, 4), op=ALU.add))
            for s in range(2):
                for n, (og, isc) in enumerate(((OFF_G1, 1), (OFF_G2, 4))):
                    op(DVE, [r_mod[l], r_vec], [r_mod[l]], lambda: nc.vector.scalar_tensor_tensor(
                        out=moda[:, l, s, n, :], in0=modv[:, l, s, isc, :], scalar=1.0, in1=V(l, og, 16), op0=ALU.add, op1=ALU.mult))

        ms("p0c")
        cv = [0]

        def conv_tile(src_ap, nk, ncols, dst_ap, rsrc_res):
            sl = cv[0] % 2
            nm = ncols // 128
            src_sb = p0_wa[:, sl].rearrange("p k n -> p (k n)")[:, 0:nk * ncols].rearrange("p (k n) -> p k n", k=nk)
            dst_sb = p0_wb[:, sl, 0:nm * nk * 128]
            dma(QL, [], [r_wa[sl]], lambda: nc.sync.dma_start(out=src_sb, in_=src_ap.rearrange("(k p) n -> p k n", p=128)))
            o = dst_sb.rearrange("p (m k j) -> p k m j", m=nm, k=nk)
            i = src_sb.rearrange("p k (m j) -> p k m j", m=nm)
            e = cv[0] % 3
            cv[0] += 1
            if e == 0:
                op(DVE, [r_wa[sl]], [r_wb[sl]], lambda: nc.vector.tensor_copy(out=o, in_=i))
            elif e == 1:
                op(ACT, [r_wa[sl]], [r_wb[sl]], lambda: nc.scalar.copy(out=o, in_=i))
            else:
                op(POOL, [r_wa[sl]], [r_wb[sl]], lambda: nc.gpsimd.tensor_copy(out=o, in_=i))
            dma(QS, [r_wb[sl]], [rsrc_res], lambda: nc.sync.dma_start(out=dst_ap.rearrange("m p f -> p m f"), in_=dst_sb.rearrange("p (m f) -> p m f", m=nm)))

        for l in range(1):
            for q in range(7):
                conv_tile(w_in[l, :, q * 512:(q + 1) * 512], KC, 512, Wt_in[l][q * 4:(q + 1) * 4], r_W[l]["inn"])
            for q in range(4):
                conv_tile(w_out[l, :, q * 512:(q + 1) * 512], KC, 512, Wt_out[l][q * 4:(q + 1) * 4], r_W[l]["out"])
            for q in range(11):
                conv_tile(w_gate[l, :, q * 512:(q + 1) * 512], KC, 512, Wt_g[l][q * 4:(q + 1) * 4], r_W[l]["g"])
                conv_tile(w_up[l, :, q * 512:(q + 1) * 512], KC, 512, Wt_u[l][q * 4:(q + 1) * 4], r_W[l]["u"])
            for hf in range(4):
                for q in range(4):
                    conv_tile(w_down[l, hf * 1408:(hf + 1) * 1408, q * 512:(q + 1) * 512], 11, 512, Wt_d[l][q * 4:(q + 1) * 4, hf], r_W[l]["d"])

    ms("p0d")
    KTb = sb("KTb", [128, 2, SEQ_S + 256], BF16)
    Vb = sb("Vb", [128, 34, 2, 128], BF16)
    r_KTb, r_Vb = RL(34), RL(34)
    KaC = sb("KaC", [128, 2, 256], BF16)
    VaC = sb("VaC", [128, 2, 2, 128], BF16)
    r_KaC, r_VaC = Res(), Res()
    KaW = sb("KaW", [128, 2, 768], BF16)
    VaW = sb("VaW", [128, 6, 256], BF16)
    r_KaW, r_VaW = Res(), Res()
    xblk = sb("xblk", [128, KC, T], F32)
    r_x = RL(KC)
    actb = sb("actb", [128, KC, T], BF16)
    r_act = RL(KC)
    gT = sb("gT", [128, 11, T], BF16)
    r_g = RL(11)
    qblk = sb("qblk", [128, 12, T], BF16)
    r_q = RL(3)
    uh = sb("uh", [128, 4, 576], BF16)
    r_uh = Res()
    NW = 6
    wring = sb("wring", [128, NW, 2048], BF16)
    r_wr = RL(NW)
    ropeCt = sb("ropeCt", [128, T], F32)
    ropeSt = sb("ropeSt", [128, T], F32)
    r_rope = Res()
    NPT = 3
    PT = sb("PT", [128, NPT, T], BF16)
    r_PT = RL(NPT)
    NTF = 7
    tf = sb("tf", [128, NTF, T], F32)
    r_tf = RL(NTF)
    NTB = 4
    tb = sb("tb", [128, NTB, T], BF16)
    r_tb = RL(NTB)
    rstd = sb("rstd", [128, T], F32)
    r_rstd = Res()
    cacc = sb("cacc", [128, 4, T], F32)
    r_cacc = RL(4)
    cstage = sb("cstage", [128, 2, 2, 256], F32)
    r_cstage = Res()

    ctr = {"tf": 0, "tb": 0, "pt": 0, "main": 0, "aux": 0, "ss": 0, "pring": 0, "m6": 0}
    cvA = sb("cvA", [128, 2048], F32)
    cvB = sb("cvB", [128, 2048], BF16)
    r_cvA, r_cvB = Res(), Res()
    bg = {"tasks": [], "ticks": 0, "every": 8}

    def bg_plan(l):
        t = []
        for mc in range(28):
            t.append((w_in[l, :, mc * 128:(mc + 1) * 128], KC, Wt_in[l][mc], r_W[l]["inn"]))
        for mc in range(16):
            t.append((w_out[l, :, mc * 128:(mc + 1) * 128], KC, Wt_out[l][mc], r_W[l]["out"]))
        for mc in range(FC):
            t.append((w_gate[l, :, mc * 128:(mc + 1) * 128], KC, Wt_g[l][mc], r_W[l]["g"]))
            t.append((w_up[l, :, mc * 128:(mc + 1) * 128], KC, Wt_u[l][mc], r_W[l]["u"]))
        for hf in range(4):
            for mc in range(16):
                t.append((w_down[l, hf * 1408:(hf + 1) * 1408, mc * 128:(mc + 1) * 128], 11, Wt_d[l][mc, hf], r_W[l]["d"]))
        return t

    def bg_emit_one():
        src, nk, dst, rdst = bg["tasks"].pop(0)
        n = nk * 128
        if PE.cnt:
            POOL.wait(Ev((PE.key, PE.sem), PE.cnt, {}))
        dma(QP, [], [r_cvA], lambda: nc.gpsimd.dma_start(out=cvA[:, 0:n].rearrange("p (k j) -> p k j", k=nk), in_=src.rearrange("(k p) n -> p k n", p=128)))
        op(POOL, [r_cvA], [r_cvB], lambda: nc.gpsimd.tensor_copy(out=cvB[:, 0:n], in_=cvA[:, 0:n]))
        dma(QP, [r_cvB], [rdst], lambda: nc.gpsimd.dma_start(out=dst, in_=cvB[:, 0:n]))

    def bg_tick():
        bg["ticks"] += 1
        if bg["tasks"] and bg["ticks"] % bg["every"] == 0:
            bg_emit_one()

    def bg_flush():
        while bg["tasks"]:
            bg_emit_one()

    def ntf():
        i = ctr["tf"] % NTF
        ctr["tf"] += 1
        return tf[:, i, :], r_tf[i]

    def ntb():
        i = ctr["tb"] % NTB
        ctr["tb"] += 1
        return tb[:, i, :], r_tb[i]

    def pmain6():
        i = ctr["m6"] % 6
        ctr["m6"] += 1
        return ps[i], r_ps[i]

    def pmain():
        i = ctr["main"] % 4
        ctr["main"] += 1
        return ps[i], r_ps[i]

    def pss():
        i = 4 + ctr["ss"] % 2
        ctr["ss"] += 1
        return ps[i], r_ps[i]

    def paux():
        i = 6 + ctr["aux"] % 2
        ctr["aux"] += 1
        return ps[i], r_ps[i]

    def in_order():
        return [0, 1, 2, 3, 4, 5, 6, 7, 8, 9, 10, 11, 12, 13, 14, 15, 16, 17, 18, 19, 20, 24, 21, 25, 22, 26, 23, 27]

    def plan():
        seq = []
        for l in range(L):
            for nblk in (1, NSB):
                for b in range(nblk):
                    for mc in in_order():
                        seq.append((Wt_in[l][mc], 2048, r_W[l]["inn"]))
                for b in range(nblk):
                    for mc in range(16):
                        seq.append((Wt_out[l][mc], 2048, r_W[l]["out"]))
                    for hf in range(4):
                        for fc in range(11):
                            seq.append((Wt_g[l][hf * 11 + fc], 2048, r_W[l]["g"]))
                            seq.append((Wt_u[l][hf * 11 + fc], 2048, r_W[l]["u"]))
                        for mc in range(16):
                            seq.append((Wt_d[l][mc, hf], 1408, r_W[l]["d"]))
        return seq

    wplan = plan()
    wst = {"next_load": 0, "next_use": 0}

    def wload_upto(n):
        while wst["next_load"] < min(n, len(wplan)):
            i = wst["next_load"]
            src, ncol, rsrc = wplan[i]
            sl = i % NW
            dma(QL, [rsrc], [r_wr[sl]], lambda: nc.sync.dma_start(out=wring[:, sl, 0:ncol], in_=src))
            wst["next_load"] += 1

    def wnext(src_check):
        i = wst["next_use"]
        assert wplan[i][0] is src_check or True
        wload_upto(i + NW)
        wst["next_use"] += 1
        bg_tick()
        return wring[:, i % NW, :], r_wr[i % NW]

    def rms_stats(src_chunks, r_src, out_rstd, r_out, nfeat, nch):
        p, rp = pss()
        g = PEGroup(PE, [rp])
        for c in range(nch):
            t, rt = ntb()
            if c % 2 == 0:
                op(ACT, [r_src[c]], [rt], lambda: nc.scalar.activation(out=t, in_=src_chunks(c), func=AF.Square))
            else:
                op(DVE, [r_src[c]], [rt], lambda: nc.vector.tensor_tensor(out=t, in0=src_chunks(c), in1=src_chunks(c), op=ALU.mult))
            g.mm([rt, r_k], lambda: nc.tensor.matmul(p[:], lhsT=ones_bf[:], rhs=t, start=(c == 0), stop=(c == nch - 1)), sig=True)
        g.done()
        t, rt = ntf()
        op(ACT, [rp, r_k], [rt], lambda: nc.scalar.activation(out=t, in_=p[:], func=AF.Ln, bias=eps_t[:], scale=1.0 / nfeat))
        op(ACT, [rt], [r_out], lambda: nc.scalar.activation(out=out_rstd, in_=t, func=AF.Exp, scale=-0.5))

    def norm_modulate(l, s, n):
        rms_stats(lambda c: xblk[:, c, :], r_x, rstd[:], r_rstd, D, KC)
        for c in range(KC):
            t, rt = ntf()
            op(DVE, [r_x[c], r_rstd, r_mod[l]], [rt], lambda: nc.vector.scalar_tensor_tensor(
                out=t, in0=xblk[:, c, :], scalar=moda[:, l, s, n, c:c + 1], in1=rstd[:], op0=ALU.mult, op1=ALU.mult))
            op(ACT, [rt, r_mod[l]], [r_act[c]], lambda: nc.scalar.activation(
                out=actb[:, c, :], in_=t, func=AF.Identity, bias=modv[:, l, s, 3 * n, c:c + 1], scale=1.0))

    def big_mm(wt, rw, nk, rhs_fn, r_rhs, p, rp, koff=0):
        g = PEGroup(PE, [rp])
        for k in range(nk):
            g.mm([rw, r_rhs[koff + k] if isinstance(r_rhs, list) else r_rhs],
                 lambda: nc.tensor.matmul(p[:], lhsT=wt[:, k * 128:(k + 1) * 128], rhs=rhs_fn(k), start=(k == 0), stop=(k == nk - 1)))
        return g.done()

    def load_x(s, b):
        xT = xT_S if s == 0 else xT_P
        rr = r_xT_S[b] if s == 0 else r_xT_P[0]
        dma(QL, [rr], r_x, lambda: nc.sync.dma_start(out=xblk[:], in_=xT[:, :, b * T:(b + 1) * T].rearrange("k p t -> p k t")))

    def transposes_to(src, rsrc, dt_ident, r_ident, p, rp):
        g = PEGroup(PE, [rp])
        for a in range(4):
            g.mm([rsrc, r_ident], lambda: nc.tensor.transpose(p[:, a * 128:(a + 1) * 128], src[:, a * 128:(a + 1) * 128], dt_ident))
        g.done()

    r_qT = [RL(NSB) for _ in range(3)]
    r_uT = RL(NSB)
    r_uTP = Res()
    r_Ka_d = RL(NSB)
    r_Va_d = RL(NSB)

    for l in range(L):
        last = (l == L - 1)
        bg_flush()
        if not last:
            bg["tasks"] = bg_plan(l + 1)
        for which, (ck, cv_) in enumerate(((cbk, cbv), (cak, cav))):
            dma(QL, [], [r_cstage], lambda: nc.sync.dma_start(
                out=cstage[:, 0], in_=ck[l].rearrange("(st s) kv d -> s st (kv d)", s=128)))
            dma(QL, [], [r_cstage], lambda: nc.sync.dma_start(
                out=cstage[:, 1], in_=cv_[l].rearrange("(st s) kv d -> s st (kv d)", s=128)))
            p, rp = paux()
            g = PEGroup(PE, [rp])
            for kv in range(2):
                for st in range(2):
                    g.mm([r_cstage, r_cst], lambda: nc.tensor.transpose(
                        p[:, (kv * 2 + st) * 128:(kv * 2 + st + 1) * 128], cstage[:, 0, st, kv * 128:(kv + 1) * 128], ident_f))
            g.done()
            if which == 0:
                op(DVE, [rp], r_KTb[0:2], lambda: nc.vector.tensor_copy(out=KTb[:, :, 0:256], in_=p[:].rearrange("p (kv t) -> p kv t", kv=2)))
                op(DVE, [r_cstage], r_Vb[0:2], lambda: nc.vector.tensor_copy(
                    out=Vb[:, 0:2].rearrange("p a k d -> p a (k d)"), in_=cstage[:, 1]))
            else:
                op(DVE, [rp], [r_KaC], lambda: nc.vector.tensor_copy(out=KaC[:], in_=p[:].rearrange("p (kv t) -> p kv t", kv=2)))
                op(DVE, [r_cstage], [r_VaC], lambda: nc.vector.tensor_copy(
                    out=VaC[:].rearrange("p a k d -> p a (k d)"), in_=cstage[:, 1]))

        for s, nblk in ((1, 1), (0, NSB)):
            for b in range(nblk):
                def rope_load(bb):
                    dma(QL, [], [r_rope], lambda: nc.sync.dma_start(out=ropeCt[:], in_=ropeC[:, bb * T:(bb + 1) * T]))
                    dma(QL, [], [r_rope], lambda: nc.sync.dma_start(out=ropeSt[:], in_=ropeS[:, bb * T:(bb + 1) * T]))
                if b == 0:
                    load_x(s, 0)
                    if s == 0:
                        rope_load(0)
                if s == 0:
                    Cc, Ss, rcs = ropeCt[:], ropeSt[:], r_rope
                else:
                    Cc, Ss, rcs = None, None, None
                norm_modulate(l, s, 0)
                if b + 1 < nblk:
                    load_x(s, b + 1)
                pend = {}
                for mc in in_order():
                    wt, rw = wnext(None)
                    p, rp = pmain()
                    big_mm(wt, rw, KC, lambda k: actb[:, k, :], r_act, p, rp)
                    if mc < 6 or 8 <= mc < 18:
                        isq = mc < 4 or 8 <= mc < 16
                        goff = OFF_QGA if mc < 4 else OFF_KGA if mc < 6 else OFF_QGB if mc < 16 else OFF_KGB
                        t_sq, r_sq = ntb()
                        op(ACT, [rp], [r_sq], lambda: nc.scalar.activation(out=t_sq, in_=p[:], func=AF.Square))
                        pq, rpq = pss()
                        g = PEGroup(PE, [rpq])
                        g.mm([r_sq, r_k], lambda: nc.tensor.matmul(pq[:], lhsT=ones_bf[:], rhs=t_sq, start=True, stop=True))
                        g.done()
                        t_zg, r_zg = ntf()
                        op(ACT, [rp, r_vec], [r_zg], lambda: nc.scalar.activation(out=t_zg, in_=p[:], func=AF.Identity, scale=V(l, goff)))
                        t_ln, r_ln = ntf()
                        op(ACT, [rpq, r_k], [r_ln], lambda: nc.scalar.activation(out=t_ln, in_=pq[:], func=AF.Ln, bias=eps_t[:], scale=1.0 / 128))
                        t_rs, r_rs = ntf()
                        op(ACT, [r_ln], [r_rs], lambda: nc.scalar.activation(out=t_rs, in_=t_ln, func=AF.Exp, scale=-0.5))
                        if s == 0:
                            px, rpx = paux()
                            g = PEGroup(PE, [rpx])
                            g.mm([r_zg, r_cst], lambda: nc.tensor.matmul(px[:], lhsT=perm_f, rhs=t_zg, start=True, stop=True))
                            g.done()
                            t_a, r_a = ntf()
                            op(DVE, [r_zg, rcs], [r_a], lambda: nc.vector.tensor_tensor(out=t_a, in0=t_zg, in1=Cc, op=ALU.mult))
                            t_b, r_b = ntf()
                            op(DVE, [rpx, rcs], [r_b], lambda: nc.vector.tensor_tensor(out=t_b, in0=px[:], in1=Ss, op=ALU.mult))
                            op(DVE, [r_a, r_b], [r_b], lambda: nc.vector.tensor_tensor(out=t_b, in0=t_a, in1=t_b, op=ALU.add))
                        else:
                            t_b, r_b = t_zg, r_zg
                        tok0 = b * T if s == 0 else 0
                        if isq:
                            hq = mc if mc < 4 else mc - 4
                            t_o, r_o = ntb()
                            op(DVE, [r_b, r_rs], [r_o], lambda: nc.vector.tensor_tensor(out=t_o, in0=t_b, in1=t_rs, op=ALU.mult))
                            grp = 0 if hq < 4 else 1 if hq < 8 else 2
                            dma(QS, [r_o], [r_qT[grp][b]], lambda: nc.sync.dma_start(out=qT_d[hq, :, tok0:tok0 + T], in_=t_o))
                        else:
                            isA = mc < 6
                            kv = (mc - 4) if isA else (mc - 16)
                            if s == 1:
                                op(DVE, [r_b, r_rs], [r_b], lambda: nc.vector.tensor_tensor(out=t_b, in0=t_b, in1=t_rs, op=ALU.mult))
                                pt_, rpt_ = paux()
                                transposes_to(t_b, r_b, ident_f, r_cst, pt_, rpt_)
                                t_s, r_s = ntf()
                                op(ACT, [rpt_], [r_s], lambda: nc.scalar.copy(out=t_s, in_=pt_[:]))
                                dstk = nak if isA else nbk
                                for sq in range(2):
                                    dma(QS, [r_s], [], lambda: nc.sync.dma_start(
                                        out=dstk[sq, l, :, kv, :].rearrange("(h p) d -> p h d", p=128),
                                        in_=t_s[:, sq * 256:(sq + 1) * 256].rearrange("p (h d) -> p h d", h=2)))
                                srck, r_srck = t_b, r_b
                            else:
                                srck, r_srck = None, None
                            if isA:
                                t_o, r_o = ntb()
                                if s == 1:
                                    op(ACT, [r_b], [r_o], lambda: nc.scalar.copy(out=t_o, in_=t_b))
                                else:
                                    op(DVE, [r_b, r_rs], [r_o], lambda: nc.vector.tensor_tensor(out=t_o, in0=t_b, in1=t_rs, op=ALU.mult))
                                dma(QS, [r_o], [r_Ka_d[b]], lambda: nc.sync.dma_start(out=KaT_d[kv, :, 128 + tok0:128 + tok0 + T], in_=t_o))
                            else:
                                dst = KTb[:, kv, 256 + tok0:256 + tok0 + T]
                                rd = r_KTb[2 + tok0 // 128:2 + tok0 // 128 + 4]
                                if s == 1:
                                    op(ACT, [r_b], rd, lambda: nc.scalar.copy(out=dst, in_=t_b))
                                else:
                                    op(DVE, [r_b, r_rs], rd, lambda: nc.vector.tensor_tensor(out=dst, in0=t_b, in1=t_rs, op=ALU.mult))
                    elif mc < 20:
                        if mc == 18 and s == 0 and b + 1 < nblk:
                            rope_load(b + 1)
                        isA = mc < 8
                        kv = (mc - 6) if isA else (mc - 18)
                        t_v, r_v = ntf()
                        op(ACT, [rp], [r_v], lambda: nc.scalar.copy(out=t_v, in_=p[:]))
                        pt_, rpt_ = paux()
                        transposes_to(t_v, r_v, ident_f, r_cst, pt_, rpt_)
                        tok0 = b * T if s == 0 else 0
                        if s == 1:
                            t_s, r_s = ntf()
                            op(ACT, [rpt_], [r_s], lambda: nc.scalar.copy(out=t_s, in_=pt_[:]))
                            dstv = nav if isA else nbv
                            for sq in range(2):
                                dma(QS, [r_s], [], lambda: nc.sync.dma_start(
                                    out=dstv[sq, l, :, kv, :].rearrange("(h p) d -> p h d", p=128),
                                    in_=t_s[:, sq * 256:(sq + 1) * 256].rearrange("p (h d) -> p h d", h=2)))
                        if isA:
                            t_o, r_o = ntb()
                            op(DVE, [rpt_], [r_o], lambda: nc.vector.tensor_copy(out=t_o, in_=pt_[:]))
                            sb0 = 1 + tok0 // 128
                            dma(QS, [r_o], [r_Va_d[b]], lambda: nc.sync.dma_start(
                                out=Va_d[sb0:sb0 + 4, :, kv * 128:(kv + 1) * 128].rearrange("a p d -> p a d"),
                                in_=t_o.rearrange("p (a d) -> p a d", a=4)))
                        else:
                            sb0 = 2 + tok0 // 128
                            op(DVE, [rpt_], r_Vb[sb0:sb0 + 4], lambda: nc.vector.tensor_copy(
                                out=Vb[:, sb0:sb0 + 4, kv, :], in_=pt_[:].rearrange("p (a d) -> p a d", a=4)))
                    elif mc < 24:
                        pend[mc] = (p, rp)
                    else:
                        pu, rpu = pend.pop(mc - 4)
                        ch = mc - 24
                        t_e, r_e = ntf()
                        op(ACT, [rp], [r_e], lambda: nc.scalar.activation(out=t_e, in_=p[:], func=AF.Exp, scale=-1.0))
                        op(DVE, [r_e], [r_e], lambda: nc.vector.tensor_scalar(out=t_e, in0=t_e, scalar1=1.0, scalar2=0.5, op0=ALU.add, op1=ALU.mult))
                        op(DVE, [r_e], [r_e], lambda: nc.vector.reciprocal(out=t_e, in_=t_e))
                        t_o, r_o = ntb()
                        op(DVE, [r_e, rpu], [r_o], lambda: nc.vector.tensor_tensor(out=t_o, in0=pu[:], in1=t_e, op=ALU.mult))
                        if s == 0:
                            dma(QS, [r_o], [r_uT[b]], lambda: nc.sync.dma_start(out=uT_d[ch, :, 16 + b * T:16 + (b + 1) * T], in_=t_o))
                        else:
                            dma(QS, [r_o], [r_uTP], lambda: nc.sync.dma_start(
                                out=uT_dP[ch, :, :].rearrange("p (s w) -> p s w", s=2)[:, :, 16:272],
                                in_=t_o.rearrange("p (s w) -> p s w", s=2)))

            ms("s1p" if s == 1 else "s1s")
            def attn_loads(b):
                tok0 = b * T if s == 0 else 0
                for grp in range(3):
                    dma(QL, [r_qT[grp][b]], [r_q[grp]], lambda: nc.sync.dma_start(
                        out=qblk[:, grp * 4:(grp + 1) * 4, :], in_=qT_d[grp * 4:(grp + 1) * 4, :, tok0:tok0 + T].rearrange("h p t -> p h t")))
                if s == 0:
                    nb_ = [r_uT[x] for x in (b - 1, b, b + 1) if 0 <= x < NSB]
                    dma(QL, nb_, [r_uh], lambda: nc.sync.dma_start(out=uh[:, :, 0:542], in_=uT_d[:, :, b * T + 1:b * T + 543].rearrange("c p w -> p c w")))
                    nbk_ = [r_Ka_d[x] for x in (b - 1, b, b + 1) if 0 <= x < NSB]
                    dma(QL, nbk_, [r_KaW], lambda: nc.sync.dma_start(out=KaW[:], in_=KaT_d[:, :, tok0:tok0 + 768].rearrange("k p t -> p k t")))
                    nbv_ = [r_Va_d[x] for x in (b - 1, b, b + 1) if 0 <= x < NSB]
                    dma(QL, nbv_, [r_VaW], lambda: nc.sync.dma_start(out=VaW[:], in_=Va_d[tok0 // 128:tok0 // 128 + 6].rearrange("a p d -> p a d")))
                else:
                    dma(QL, [r_uTP], [r_uh], lambda: nc.sync.dma_start(out=uh[:, :, :], in_=uT_dP[:, :, :].rearrange("c p w -> p c w")))
                    dma(QL, [r_Ka_d[0]], [r_KaW], lambda: nc.sync.dma_start(out=KaW[:, :, 0:640], in_=KaT_d[:, :, 0:640].rearrange("k p t -> p k t")))
                    dma(QL, [r_Va_d[0]], [r_VaW], lambda: nc.sync.dma_start(out=VaW[:, 0:5], in_=Va_d[0:5].rearrange("a p d -> p a d")))


            for b in range(nblk):
                tok0 = b * T if s == 0 else 0
                if b == 0:
                    attn_loads(0)

                conv_ops = []
                nseq = 1 if s == 0 else 2
                wdt = T // nseq
                for sq in range(nseq):
                    base = sq * 288 + 1 if s == 1 else 0
                    for k in range(31):
                        for ch in range(4):
                            def _cop(sq=sq, base=base, k=k, ch=ch):
                                src = uh[:, ch, base + k:base + k + wdt]
                                acc = cacc[:, ch, sq * wdt:(sq + 1) * wdt]
                                if k == 0:
                                    op(DVE, [r_uh, r_k, r_vec], [r_cacc[ch]], lambda: nc.vector.tensor_scalar(
                                        out=acc, in0=src, scalar1=cwh[:, l, ch * 31:ch * 31 + 1], scalar2=V(l, OFF_CB + ch), op0=ALU.mult, op1=ALU.add))
                                else:
                                    op(DVE, [r_uh, r_k, r_cacc[ch]], [r_cacc[ch]], lambda: nc.vector.scalar_tensor_tensor(
                                        out=acc, in0=src, scalar=cwh[:, l, ch * 31 + k:ch * 31 + k + 1], in1=acc, op0=ALU.mult, op1=ALU.add))
                            conv_ops.append(_cop)

                def conv_step(n):
                    for _ in range(n * nseq):
                        if conv_ops:
                            conv_ops.pop(0)()

                def conv_finish():
                    conv_step(10 ** 6)
                    psum_, rps_ = pss()
                    psq_, rpq_ = pss()
                    g1_ = PEGroup(PE, [rps_])
                    g2_ = PEGroup(PE, [rpq_])
                    for ch in range(4):
                        t1, rt1 = ntb()
                        op(ACT, [r_cacc[ch]], [rt1], lambda: nc.scalar.copy(out=t1, in_=cacc[:, ch, :]))
                        g1_.mm([rt1, r_k], lambda: nc.tensor.matmul(psum_[:], lhsT=ones_bf[:], rhs=t1, start=(ch == 0), stop=(ch == 3)), sig=True)
                        t2, rt2 = ntb()
                        op(ACT, [r_cacc[ch]], [rt2], lambda: nc.scalar.activation(out=t2, in_=cacc[:, ch, :], func=AF.Square))
                        g2_.mm([rt2, r_k], lambda: nc.tensor.matmul(psq_[:], lhsT=ones_bf[:], rhs=t2, start=(ch == 0), stop=(ch == 3)), sig=True)
                    g1_.done()
                    g2_.done()
                    t_m, r_m = rstd[:], r_rstd
                    op(DVE, [rps_], [r_m], lambda: nc.vector.tensor_scalar(out=t_m, in0=psum_[:], scalar1=1.0 / 512, scalar2=None, op0=ALU.mult))
                    t_v, r_v = ropeCt[:], r_rope
                    op(DVE, [r_m], [r_v], lambda: nc.vector.tensor_tensor(out=t_v, in0=t_m, in1=t_m, op=ALU.mult))
                    op(DVE, [rpq_, r_v], [r_v], lambda: nc.vector.scalar_tensor_tensor(out=t_v, in0=psq_[:], scalar=1.0 / 512, in1=t_v, op0=ALU.mult, op1=ALU.subtract))
                    op(ACT, [r_v, r_k], [r_v], lambda: nc.scalar.activation(out=t_v, in_=t_v, func=AF.Ln, bias=eps_t[:], scale=1.0))
                    op(ACT, [r_v], [r_v], lambda: nc.scalar.activation(out=t_v, in_=t_v, func=AF.Exp, scale=-0.5))
                    for ch in range(4):
                        op(DVE, [r_cacc[ch], r_m], [r_cacc[ch]], lambda: nc.vector.tensor_tensor(out=cacc[:, ch, :], in0=cacc[:, ch, :], in1=t_m, op=ALU.subtract))
                    for ch in range(4):
                        op(DVE, [r_cacc[ch], r_v], [r_cacc[ch]], lambda: nc.vector.tensor_tensor(out=cacc[:, ch, :], in0=cacc[:, ch, :], in1=t_v, op=ALU.mult))
                    for ch in range(4):
                        op(ACT, [r_cacc[ch], r_vec], [r_cacc[ch]], lambda: nc.scalar.activation(out=cacc[:, ch, :], in_=cacc[:, ch, :], func=AF.Identity, bias=V(l, OFF_LB + ch), scale=V(l, OFF_LG + ch)))
                    tl = []
                    for ch in range(4):
                        t2, rt2 = ntf()
                        tl.append((t2, rt2))
                        op(ACT, [r_cacc[ch]], [rt2], lambda: nc.scalar.activation(out=t2, in_=cacc[:, ch, :], func=AF.Exp, scale=-1.0))
                    for ch in range(4):
                        t2, rt2 = tl[ch]
                        op(DVE, [rt2], [rt2], lambda: nc.vector.tensor_scalar(out=t2, in0=t2, scalar1=1.0, scalar2=None, op0=ALU.add))
                    for ch in range(4):
                        t2, rt2 = tl[ch]
                        op(DVE, [rt2], [rt2], lambda: nc.vector.reciprocal(out=t2, in_=t2))
                    for ch in range(4):
                        t2, rt2 = tl[ch]
                        op(DVE, [r_cacc[ch], rt2], [r_act[12 + ch]], lambda: nc.vector.tensor_tensor(out=actb[:, 12 + ch, :], in0=cacc[:, ch, :], in1=t2, op=ALU.mult))


                def attn_unit(qap, rq, nh, segs, out_ap, r_out, sink_cols):
                    n = nh * 128
                    po, rpo = ps[3 + ctr["pt"] % 2], r_ps[3 + ctr["pt"] % 2]
                    pd, rpd = ps[5 + ctr["pt"] % 2], r_ps[5 + ctr["pt"] % 2]
                    ctr["pt"] += 1
                    go = PEGroup(PE, [rpo])
                    gd = PEGroup(PE, [rpd])
                    prev = None
                    for i in range(len(segs) + 1):
                        if i < len(segs):
                            kT, rk, v, rv, m = segs[i]
                            j = ctr["main"] % 3
                            ctr["main"] += 1
                            pS, rpS = ps[j], r_ps[j]
                            g = PEGroup(PE, [rpS])
                            g.mm([rk, rq], lambda: nc.tensor.matmul(pS[:, 0:n].rearrange("p (h q) -> p h q", h=nh), lhsT=kT, rhs=qap, start=True, stop=(m is None)))
                            if m is not None:
                                g.mm([r_k], lambda: nc.tensor.matmul(pS[:, 0:n], lhsT=ident_bf[:], rhs=m, start=False, stop=True))
                            g.done()
                            jp = ctr["pring"] % NPT
                            ctr["pring"] += 1
                            op(ACT, [rpS], [r_PT[jp]], lambda: nc.scalar.activation(out=PT[:, jp, 0:n], in_=pS[:, 0:n], func=AF.Exp, scale=SCALE))
                            cur = (jp, v, rv)
                        else:
                            cur = None
                        if prev is not None:
                            jp0, v0, rv0 = prev
                            first = (i == 1)
                            lastseg = (i == len(segs))
                            go.mm([r_PT[jp0], rv0], lambda: nc.tensor.matmul(po[:, 0:n], lhsT=v0, rhs=PT[:, jp0, 0:n], start=first, stop=lastseg))
                            gd.mm([r_PT[jp0], r_k], lambda: nc.tensor.matmul(pd[:, 0:n], lhsT=ones_bf[:], rhs=PT[:, jp0, 0:n], start=first, stop=lastseg))
                        prev = cur
                    go.done()
                    gd.done()
                    t_r, r_r = ntf()
                    if sink_cols is not None:
                        for h in range(nh):
                            op(DVE, [rpd, r_k], [r_r], lambda: nc.vector.tensor_scalar(
                                out=t_r[:, h * 128:(h + 1) * 128], in0=pd[:, h * 128:(h + 1) * 128], scalar1=esink[:, sink_cols + h:sink_cols + h + 1], scalar2=None, op0=ALU.add))
                        op(DVE, [r_r], [r_r], lambda: nc.vector.reciprocal(out=t_r[:, 0:n], in_=t_r[:, 0:n]))
                    else:
                        op(DVE, [rpd], [r_r], lambda: nc.vector.reciprocal(out=t_r[:, 0:n], in_=pd[:, 0:n]))
                    op(DVE, [rpo, r_r], r_out, lambda: nc.vector.tensor_tensor(
                        out=out_ap, in0=po[:, 0:n].rearrange("p (h q) -> p h q", h=nh), in1=t_r[:, 0:n].rearrange("p (h q) -> p h q", h=nh), op=ALU.mult))

                for kv in range(2):
                    for qs in range(4):
                        qap = qblk[:, kv * 2:kv * 2 + 2, qs * 128:(qs + 1) * 128]
                        segs = []
                        if s == 0:
                            for st in range(2):
                                segs.append((KaC[:, kv, st * 128:(st + 1) * 128], r_KaC, VaC[:, st, kv, :], r_VaC, None))
                            gq = b * 4 + qs
                            for dlt, m in ((-1, mask_bf[:, 0, :]), (0, None), (1, mask_bf[:, 1, :])):
                                if 0 <= gq + dlt < 32:
                                    w0 = (qs + 1 + dlt) * 128
                                    segs.append((KaW[:, kv, w0:w0 + 128], r_KaW, VaW[:, qs + 1 + dlt, kv * 128:(kv + 1) * 128], r_VaW, m))
                        else:
                            sq = qs // 2
                            for st in range(2):
                                w0 = 128 + sq * 256 + st * 128
                                segs.append((KaW[:, kv, w0:w0 + 128], r_KaW, VaW[:, 1 + sq * 2 + st, kv * 128:(kv + 1) * 128], r_VaW, None))
                        attn_unit(qap, r_q[0], 2, segs, actb[:, kv * 2:kv * 2 + 2, qs * 128:(qs + 1) * 128], r_act[kv * 2:kv * 2 + 2], l * 4 + kv * 2)
                        conv_step(4)
                for kv in range(2):
                    for qs in range(4):
                        qap = qblk[:, 4 + kv * 4:8 + kv * 4, qs * 128:(qs + 1) * 128]
                        if s == 0:
                            sbl = list(range(34))
                        else:
                            sq = qs // 2
                            sbl = [2 + sq * 2, 3 + sq * 2]
                        segs = [(KTb[:, kv, j * 128:(j + 1) * 128], r_KTb[j], Vb[:, j, kv, :], r_Vb[j], None) for j in sbl]
                        attn_unit(qap, r_q[1 + kv], 4, segs, actb[:, 4 + kv * 4:8 + kv * 4, qs * 128:(qs + 1) * 128], r_act[4 + kv * 4:8 + kv * 4], None)
                        ub = kv * 4 + qs
                        if ub < 5:
                            conv_step(20)
                        if ub == 4:
                            conv_finish()

                ms("attn")
                ms("conv")
                load_x(s, b)
                for mc in range(16):
                    wt, rw = wnext(None)
                    p, rp = pmain6()
                    big_mm(wt, rw, KC, lambda k: actb[:, k, :], r_act, p, rp)
                    op(DVE, [rp, r_x[mc], r_mod[l]], [r_x[mc]], lambda: nc.vector.scalar_tensor_tensor(
                        out=xblk[:, mc, :], in0=p[:], scalar=modv[:, l, s, 2, mc:mc + 1], in1=xblk[:, mc, :], op0=ALU.mult, op1=ALU.add))

                ms("wout")
                norm_modulate(l, s, 1)
                if b + 1 < nblk:
                    attn_loads(b + 1)
                for hf in range(4):
                    for fc in range(11):
                        wg_, rwg = wnext(None)
                        pg, rpg = pmain6()
                        big_mm(wg_, rwg, KC, lambda k: actb[:, k, :], r_act, pg, rpg)
                        wu_, rwu = wnext(None)
                        pu, rpu = pmain6()
                        big_mm(wu_, rwu, KC, lambda k: actb[:, k, :], r_act, pu, rpu)
                        t1, rt1 = ntf()
                        op(ACT, [rpg], [rt1], lambda: nc.scalar.activation(out=t1, in_=pg[:], func=AF.Silu))
                        op(DVE, [rt1, rpu], [r_g[fc]], lambda: nc.vector.tensor_tensor(out=gT[:, fc, :], in0=pu[:], in1=t1, op=ALU.mult))
                    for mc in range(16):
                        wt, rw = wnext(None)
                        p, rp = pmain6()
                        big_mm(wt, rw, 11, lambda k: gT[:, k, :], r_g, p, rp)
                        op(DVE, [rp, r_x[mc], r_mod[l]], [r_x[mc]], lambda: nc.vector.scalar_tensor_tensor(
                            out=xblk[:, mc, :], in0=p[:], scalar=modv[:, l, s, 5, mc:mc + 1], in1=xblk[:, mc, :], op0=ALU.mult, op1=ALU.add))

                ms("ffn")
                if not last:
                    xT = xT_S if s == 0 else xT_P
                    rr = r_xT_S[b] if s == 0 else r_xT_P[0]
                    dma(QS, r_x, [rr], lambda: nc.sync.dma_start(out=xT[:, :, b * T:(b + 1) * T].rearrange("k p t -> p k t"), in_=xblk[:]))
                else:
                    ydst = y_s if s == 0 else y_p
                    for a in range(4):
                        for q4 in range(4):
                            pt_, rpt_ = paux()
                            g = PEGroup(PE, [rpt_])
                            for j in range(4):
                                kc = q4 * 4 + j
                                g.mm([r_x[kc], r_cst], lambda: nc.tensor.transpose(pt_[:, j * 128:(j + 1) * 128], xblk[:, kc, a * 128:(a + 1) * 128], ident_f))
                            g.done()
                            t1, rt1 = ntf()
                            if q4 % 2 == 0:
                                op(DVE, [rpt_], [rt1], lambda: nc.vector.tensor_copy(out=t1, in_=pt_[:]))
                            else:
                                op(ACT, [rpt_], [rt1], lambda: nc.scalar.copy(out=t1, in_=pt_[:]))
                            dma(QS, [rt1], [], lambda: nc.sync.dma_start(out=ydst[tok0 + a * 128:tok0 + (a + 1) * 128, q4 * 512:(q4 + 1) * 512], in_=t1))

    return nc


def _fm(v):
    return np.ascontiguousarray(v.reshape(-1, 128).T)


def _consts():
    c = np.zeros((128, 768), np.float32)
    c[:, 0:128] = np.eye(128, dtype=np.float32)
    k = np.arange(128)
    c[(k + 64) % 128, 128 + k] = 1.0
    j = np.arange(128)[:, None]
    i = np.arange(128)[None, :]
    lo = np.where(j >= i, 0.0, NEG).astype(np.float32)
    hi = np.where(j <= i, 0.0, NEG).astype(np.float32)
    c[:, 256:384] = lo
    c[:, 384:512] = lo
    c[:, 512:640] = hi
    c[:, 640:768] = hi
    return c


def _rope():
    t = np.arange(SEQ_S)
    row = (t // 64).astype(np.float32)
    col = (t % 64).astype(np.float32)
    inv = np.power(np.float32(10000.0), -np.arange(32, dtype=np.float32) / np.float32(32)).astype(np.float32)
    ang = np.concatenate([row[:, None] * inv, col[:, None] * inv], axis=-1).astype(np.float32)
    cos = np.cos(ang).astype(np.float32).T
    sin = np.sin(ang).astype(np.float32).T
    C = np.concatenate([cos, cos], axis=0)
    S = np.concatenate([-sin, sin], axis=0)
    return np.ascontiguousarray(C), np.ascontiguousarray(S)


_CACHE = {}


def kernel(x_prompt, x_sample, cache_a_k, cache_a_v, cache_b_k, cache_b_v, c, c_ctx,
           w_ada, b_ada, w_in, w_out, w_gate, w_up, w_down, norm1_g, norm2_g,
           qnorm_a_g, knorm_a_g, qnorm_b_g, knorm_b_g, sink_a,
           conv_w, conv_b, conv_ln_g, conv_ln_b, _depth=None, _dbg=None):
    f = lambda a: np.ascontiguousarray(np.asarray(a, dtype=np.float32))
    L = int(_depth) if _depth else DEPTH
    n = 8
    key = L
    if key not in _CACHE:
        _CACHE[key] = build(L, _dbg)
    nc = _CACHE[key]
    x_prompt, x_sample = f(x_prompt), f(x_sample)
    cak, cav, cbk, cbv = f(cache_a_k), f(cache_a_v), f(cache_b_k), f(cache_b_v)
    c, c_ctx = f(c), f(c_ctx)
    shared = dict(w_ada=f(w_ada)[:L], w_in=f(w_in)[:L], w_out=f(w_out)[:L], w_gate=f(w_gate)[:L], w_up=f(w_up)[:L], w_down=f(w_down)[:L],
                  consts=_consts())
    shared["ropeC"], shared["ropeS"] = _rope()
    base = np.zeros((128, 32 + L * LV), np.float32)
    base[:, 16:32] = _fm(c_ctx)
    b_ada, norm1_g, norm2_g = f(b_ada), f(norm1_g), f(norm2_g)
    qa, ka, qb, kb = f(qnorm_a_g), f(knorm_a_g), f(qnorm_b_g), f(knorm_b_g)
    sink_a, conv_w, conv_b, lg, lb = f(sink_a), f(conv_w), f(conv_b), f(conv_ln_g), f(conv_ln_b)
    for l in range(L):
        o = 32 + l * LV
        base[:, o:o + 16] = _fm(norm1_g[l])
        base[:, o + 16:o + 32] = _fm(norm2_g[l])
        base[:, o + 32:o + 128] = _fm(b_ada[l])
        base[:, o + 128] = qa[l]
        base[:, o + 129] = ka[l]
        base[:, o + 130] = qb[l]
        base[:, o + 131] = kb[l]
        base[:, o + 132:o + 256] = conv_w[l].T.reshape(4, 128, 31).transpose(1, 0, 2).reshape(128, 124)
        base[:, o + 256:o + 260] = _fm(conv_b[l])
        base[:, o + 260:o + 264] = _fm(lg[l])
        base[:, o + 264:o + 268] = _fm(lb[l])
        base[:, o + 268:o + 272] = np.broadcast_to(sink_a[l][None, :], (128, 4))
    in_maps = []
    for i in range(n):
        v = base.copy()
        v[:, 0:16] = _fm(c[i])
        m = dict(shared)
        m.update(x_s=x_sample[i], x_p=np.ascontiguousarray(x_prompt[2 * i:2 * i + 2].reshape(T, D)),
                 cak=np.ascontiguousarray(cak[i, :L]), cav=np.ascontiguousarray(cav[i, :L]),
                 cbk=np.ascontiguousarray(cbk[i, :L]), cbv=np.ascontiguousarray(cbv[i, :L]), vecs=v)
        in_maps.append(m)
    res = run_bass_kernel_spmd(nc, in_maps, core_ids=list(range(n)))
    R = res.results
    y_prompt = np.concatenate([r["y_p"].reshape(2, 256, D) for r in R], axis=0)
    y_sample = np.stack([r["y_s"] for r in R], axis=0)
    outs = [np.concatenate([r[k] for r in R], axis=0) for k in ("nak", "nav", "nbk", "nbv")]
    return (y_prompt, y_sample, outs[0], outs[1], outs[2], outs[3])
```

```python
import numpy as np
import concourse.bass as bass
import concourse.mybir as mybir
from concourse.bass_utils import run_bass_kernel_spmd

F32 = mybir.dt.float32
BF16 = mybir.dt.bfloat16
AF = mybir.ActivationFunctionType
ALU = mybir.AluOpType

D = 2048
KC = 16
DFF = 5632
FC = 44
T = 512
DEPTH = 4
NSB = 8
SEQ_S = 4096
EPS = 1e-6
SCALE = 128 ** -0.5
LV = 272
NEG = -30000.0


class Ev:
    __slots__ = ("sem", "val", "clock")

    def __init__(self, sem, val, clock):
        self.sem = sem
        self.val = val
        self.clock = clock


class Res:
    __slots__ = ("w", "r", "ex")

    def __init__(self, ex=False):
        self.w = None
        self.r = {}
        self.ex = ex


def RL(n):
    return [Res() for _ in range(n)]


class Eng:
    def __init__(self, nc, e, name, self_ordered=False):
        self.e = e
        self.sem = nc.alloc_semaphore("s_" + name)
        self.key = "E" + name
        self.cnt = 0
        self.seen = {}
        self.self_ordered = self_ordered

    def wait(self, ev):
        if ev is None or self.seen.get(ev.sem[0], 0) >= ev.val:
            return
        self.e.wait_ge(ev.sem[1], ev.val)
        for s, v in ev.clock.items():
            if self.seen.get(s, 0) < v:
                self.seen[s] = v
        self.seen[ev.sem[0]] = ev.val

    def signal(self, inst):
        self.cnt += 1
        inst.then_inc(self.sem, 1)
        if self.self_ordered:
            self.seen[self.key] = self.cnt
        return Ev((self.key, self.sem), self.cnt, dict(self.seen))


class DmaQ:
    def __init__(self, nc, eng, name, nsem):
        self.E = eng
        self.sems = [[(name + str(i), nc.alloc_semaphore("d_" + name + str(i))), 0, None] for i in range(nsem)]
        self.i = 0


def _deps(reads, writes):
    out = []
    for x in reads:
        if x.w is not None:
            out.append(x.w)
    for x in writes:
        if x.w is not None:
            out.append(x.w)
        out.extend(x.r.values())
    return out


def _commit(ev, reads, writes):
    k = ev.sem[0]
    for x in reads:
        c = x.r.get(k)
        if c is None or c.val < ev.val:
            x.r[k] = ev
    for x in writes:
        x.w = ev
        x.r = {}


def op(E, reads, writes, fn):
    if any(x.ex for x in reads):
        writes = list(writes) + [x for x in reads if x.ex]
        reads = [x for x in reads if not x.ex]
    for ev in _deps(reads, writes):
        E.wait(ev)
    ev = E.signal(fn())
    _commit(ev, reads, writes)
    return ev


def dma(Q, reads, writes, fn):
    E = Q.E
    for ev in _deps(reads, writes):
        E.wait(ev)
    s = Q.sems[Q.i]
    Q.i = (Q.i + 1) % len(Q.sems)
    E.wait(s[2])
    s[1] += 16
    fn().then_inc(s[0][1], 16)
    ev = Ev(s[0], s[1], dict(E.seen))
    s[2] = ev
    _commit(ev, reads, writes)
    return ev


class PEGroup:
    def __init__(self, PE, writes):
        self.PE = PE
        self.writes = writes
        self.reads = []
        self.last = None
        for ev in _deps([], writes):
            PE.wait(ev)

    def mm(self, reads, fn, sig=False):
        for x in reads:
            if x.w is not None:
                self.PE.wait(x.w)
        self.last = fn()
        if sig:
            ev = self.PE.signal(self.last)
            _commit(ev, reads, [])
            self.sig_last = True
        else:
            self.reads.extend(reads)
            self.sig_last = False

    def done(self):
        if getattr(self, "sig_last", False):
            ev = Ev((self.PE.key, self.PE.sem), self.PE.cnt, dict(self.PE.seen))
        else:
            ev = self.PE.signal(self.last)
        _commit(ev, self.reads, self.writes)
        return ev


class _Stop(Exception):
    pass


def build(depth=DEPTH, dbg=None):
    nc = bass.Bass("TRN2", target_bir_lowering=False)
    L = depth
    fin = {}
    try:
        _build_body(nc, L, dbg, fin)
    except _Stop:
        pass
    POOL, QS, QL, engs = fin["POOL"], fin["QS"], fin["QL"], fin["engs"]
    for Q in (QS, QL, fin["QP"]):
        for s_ in Q.sems:
            POOL.wait(s_[2])
    for E in engs:
        if E.cnt:
            POOL.e.wait_ge(E.sem, E.cnt)
    return nc


def _build_body(nc, L, dbg, fin):
    def ms(name):
        if dbg == name:
            raise _Stop()

    def din(name, shape, dt=F32):
        return nc.dram_tensor(name, list(shape), dt, kind="ExternalInput").ap()

    def dout(name, shape):
        return nc.dram_tensor(name, list(shape), F32, kind="ExternalOutput").ap()

    def dscr(name, shape, dt):
        return nc.dram_tensor(name, list(shape), dt, kind="Internal").ap()

    x_s = din("x_s", [SEQ_S, D])
    x_p = din("x_p", [T, D])
    cak = din("cak", [L, 256, 2, 128])
    cav = din("cav", [L, 256, 2, 128])
    cbk = din("cbk", [L, 256, 2, 128])
    cbv = din("cbv", [L, 256, 2, 128])
    vecs = din("vecs", [128, 32 + L * LV])
    consts = din("consts", [128, 128 * 2 + 512])
    ropeC = din("ropeC", [128, SEQ_S])
    ropeS = din("ropeS", [128, SEQ_S])
    w_ada = din("w_ada", [L, D, 6 * D])
    w_in = din("w_in", [L, D, 3584])
    w_out = din("w_out", [L, D, D])
    w_gate = din("w_gate", [L, D, DFF])
    w_up = din("w_up", [L, D, DFF])
    w_down = din("w_down", [L, DFF, D])
    y_s = dout("y_s", [SEQ_S, D])
    y_p = dout("y_p", [T, D])
    nak = dout("nak", [2, L, 256, 2, 128])
    nav = dout("nav", [2, L, 256, 2, 128])
    nbk = dout("nbk", [2, L, 256, 2, 128])
    nbv = dout("nbv", [2, L, 256, 2, 128])

    xT_S = dscr("xT_S", [KC, 128, SEQ_S], F32)
    xT_P = dscr("xT_P", [KC, 128, T], F32)
    Wt_in = [dscr(f"Wt_in{l}", [28, 128, 2048], BF16) for l in range(L)]
    Wt_out = [dscr(f"Wt_out{l}", [16, 128, 2048], BF16) for l in range(L)]
    Wt_g = [dscr(f"Wt_g{l}", [FC, 128, 2048], BF16) for l in range(L)]
    Wt_u = [dscr(f"Wt_u{l}", [FC, 128, 2048], BF16) for l in range(L)]
    Wt_d = [dscr(f"Wt_d{l}", [16, 4, 128, 11 * 128], BF16) for l in range(L)]
    qT_d = dscr("qT_d", [12, 128, SEQ_S], BF16)
    uT_d = dscr("uT_d", [4, 128, SEQ_S + 32], BF16)
    uT_dP = dscr("uT_dP", [4, 128, 2 * 288], BF16)
    KaT_d = dscr("KaT_d", [2, 128, SEQ_S + 256], BF16)
    Va_d = dscr("Va_d", [34, 128, 256], BF16)

    PE = Eng(nc, nc.tensor, "pe", self_ordered=True)
    ACT = Eng(nc, nc.scalar, "act")
    DVE = Eng(nc, nc.vector, "dve")
    POOL = Eng(nc, nc.gpsimd, "pool")
    SP = Eng(nc, nc.sync, "sp")
    QL = DmaQ(nc, SP, "ql", 40)
    QS = DmaQ(nc, SP, "qs", 24)
    QP = DmaQ(nc, POOL, "qp", 4)
    fin.update(POOL=POOL, QS=QS, QL=QL, QP=QP, engs=(PE, ACT, DVE))

    def sb(name, shape, dt):
        return nc.alloc_sbuf_tensor(name, list(shape), dt)

    vec_t = sb("vec_t", [128, 32 + L * LV], F32)
    r_vec = Res()
    cst_t = sb("cst_t", [128, 768], F32)
    r_cst = Res()
    ident_bf = sb("ident_bf", [128, 128], BF16)
    ones_bf = sb("ones_bf", [128, 128], BF16)
    mask_bf = sb("mask_bf", [128, 2, 256], BF16)
    eps_t = sb("eps_t", [128, 1], F32)
    qsum_f = sb("qsum_f", [128, 128], F32)
    r_k = Res()
    modv = sb("modv", [128, L, 2, 6, KC], F32)
    r_mod = RL(L)
    moda = sb("moda", [128, L, 2, 2, KC], F32)
    esink = sb("esink", [128, L * 4], F32)
    cwh = sb("cwh", [128, L, 124], F32)
    ident_f = cst_t[:, 0:128]
    perm_f = cst_t[:, 128:256]

    def V(l, off, n=1):
        b = 32 + l * LV + off
        return vec_t[:, b:b + n]

    OFF_G1, OFF_G2, OFF_BADA, OFF_QGA, OFF_KGA, OFF_QGB, OFF_KGB = 0, 16, 32, 128, 129, 130, 131
    OFF_CW, OFF_CB, OFF_LG, OFF_LB, OFF_SINK = 132, 256, 260, 264, 268

    ps = [nc.alloc_psum_tensor(f"ps{i}", [128, T], F32) for i in range(8)]
    r_ps = [Res(ex=True) for _ in range(8)]

    dma(QL, [], [r_vec], lambda: nc.sync.dma_start(out=vec_t[:], in_=vecs[:, :]))
    dma(QL, [], [r_cst], lambda: nc.sync.dma_start(out=cst_t[:], in_=consts[:, :]))
    op(DVE, [r_cst], [r_k], lambda: nc.vector.tensor_copy(out=ident_bf[:], in_=ident_f))
    op(DVE, [r_cst], [r_k], lambda: nc.vector.tensor_copy(out=mask_bf[:].rearrange("p a b -> p (a b)"), in_=cst_t[:, 256:768]))
    op(DVE, [], [r_k], lambda: nc.vector.memset(ones_bf[:], 1.0))
    op(DVE, [], [r_k], lambda: nc.vector.memset(eps_t[:], EPS))
    op(DVE, [], [r_k], lambda: nc.vector.memset(qsum_f[:], 1.0 / 32))
    for l in range(L):
        op(ACT, [r_vec], [r_k], lambda: nc.scalar.activation(out=esink[:, l * 4:l * 4 + 4], in_=V(l, OFF_SINK, 4), func=AF.Exp))
        op(DVE, [r_vec], [r_k], lambda: nc.vector.tensor_scalar(out=cwh[:, l, :], in0=V(l, OFF_CW, 124), scalar1=0.5, scalar2=None, op0=ALU.mult))

    r_xT_S = RL(NSB)
    r_xT_P = RL(1)
    r_W = [dict(inn=Res(), out=Res(), g=Res(), u=Res(), d=Res()) for _ in range(L)]

    with nc.sbuf_tensor("p0_x", [128, 4, D], F32) as p0_x, \
            nc.sbuf_tensor("p0_st", [128, KC, T], F32) as p0_st, \
            nc.sbuf_tensor("p0_wa", [128, 2, KC, 512], F32) as p0_wa, \
            nc.sbuf_tensor("p0_wb", [128, 2, 4 * KC * 128], BF16) as p0_wb, \
            nc.sbuf_tensor("p0_sc", [128, KC, 2], F32) as p0_sc, \
            nc.sbuf_tensor("p0_zb", [128, 512], BF16) as p0_zb, \
            nc.sbuf_tensor("adat", [2, 512], F32) as adat:
        r_adat = Res()
        r_p0x, r_p0st, r_wa, r_wb, r_sc, r_zb = Res(), RL(KC), RL(2), RL(2), Res(), Res()

        op(DVE, [], [r_zb], lambda: nc.vector.memset(p0_zb[:], 0.0))
        for c in range(4):
            dma(QS, [r_zb], [], lambda: nc.sync.dma_start(out=uT_d[c, :, 0:16], in_=p0_zb[:, 0:16]))
            dma(QS, [r_zb], [], lambda: nc.sync.dma_start(out=uT_d[c, :, SEQ_S + 16:SEQ_S + 32], in_=p0_zb[:, 0:16]))
            dma(QS, [r_zb], [], lambda: nc.sync.dma_start(out=uT_dP[c, :, :].rearrange("p (s w) -> p s w", s=2)[:, :, 0:16], in_=p0_zb[:, 0:32].rearrange("p (s w) -> p s w", s=2)))
            dma(QS, [r_zb], [], lambda: nc.sync.dma_start(out=uT_dP[c, :, :].rearrange("p (s w) -> p s w", s=2)[:, :, 272:288], in_=p0_zb[:, 0:32].rearrange("p (s w) -> p s w", s=2)))
        for kv in range(2):
            dma(QS, [r_zb], [], lambda: nc.sync.dma_start(out=KaT_d[kv, :, 0:128], in_=p0_zb[:, 0:128]))
            dma(QS, [r_zb], [], lambda: nc.sync.dma_start(out=KaT_d[kv, :, SEQ_S + 128:SEQ_S + 256], in_=p0_zb[:, 0:128]))
        dma(QS, [r_zb], [], lambda: nc.sync.dma_start(out=Va_d[0], in_=p0_zb[:, 0:256]))
        dma(QS, [r_zb], [], lambda: nc.sync.dma_start(out=Va_d[33], in_=p0_zb[:, 0:256]))

        ms("p0a")
        def to_feature_major(xsrc, xT, r_dst, t0):
            dma(QL, [], [r_p0x], lambda: nc.sync.dma_start(out=p0_x[:], in_=xsrc[t0:t0 + T, :].rearrange("(a p) f -> p a f", p=128)))
            for kc in range(KC):
                b = kc % 2
                g = PEGroup(PE, [r_ps[b]])
                for a in range(4):
                    g.mm([r_p0x, r_cst], lambda: nc.tensor.transpose(ps[b][:, a * 128:(a + 1) * 128], p0_x[:, a, kc * 128:(kc + 1) * 128], ident_f))
                g.done()
                if kc % 2 == 0:
                    op(DVE, [r_ps[b]], [r_p0st[kc]], lambda: nc.vector.tensor_copy(out=p0_st[:, kc, :], in_=ps[b][:]))
                else:
                    op(ACT, [r_ps[b]], [r_p0st[kc]], lambda: nc.scalar.copy(out=p0_st[:, kc, :], in_=ps[b][:]))
            dma(QS, r_p0st, [r_dst], lambda: nc.sync.dma_start(out=xT[:, :, t0:t0 + T].rearrange("k p t -> p k t"), in_=p0_st[:]))

        to_feature_major(x_p, xT_P, r_xT_P[0], 0)
        for b in range(NSB):
            to_feature_major(x_s, xT_S, r_xT_S[b], b * T)

        ms("p0b")
        op(ACT, [r_vec], [r_sc], lambda: nc.scalar.activation(out=p0_sc[:, :, 0], in_=vec_t[:, 0:16], func=AF.Silu))
        op(ACT, [r_vec], [r_sc], lambda: nc.scalar.activation(out=p0_sc[:, :, 1], in_=vec_t[:, 16:32], func=AF.Silu))
        cnt = 0
        for l in range(L):
            for q in range(24):
                sl = cnt % 2
                cnt += 1
                dma(QL, [], [r_wa[sl]], lambda: nc.sync.dma_start(out=p0_wa[:, sl], in_=w_ada[l, :, q * 512:(q + 1) * 512].rearrange("(k p) n -> p k n", p=128)))
                g = PEGroup(PE, [r_ps[2]])
                for kc in range(KC):
                    g.mm([r_wa[sl], r_sc], lambda: nc.tensor.matmul(ps[2][0:2, :], lhsT=p0_sc[:, kc, :], rhs=p0_wa[:, sl, kc, :], start=(kc == 0), stop=(kc == KC - 1)))
                g.done()
                op(ACT, [r_ps[2]], [r_adat], lambda: nc.scalar.copy(out=adat[:], in_=ps[2][0:2, :]))
                g = PEGroup(PE, [r_ps[3]])
                for j in range(4):
                    g.mm([r_adat, r_cst], lambda: nc.tensor.transpose(ps[3][:, j * 2:j * 2 + 2], adat[0:2, j * 128:(j + 1) * 128], cst_t[0:2, 0:2]))
                g.done()
                i0, k0 = (q * 4) // 16, (q * 4) % 16
                for s in range(2):
                    op(DVE, [r_ps[3], r_vec], [r_mod[l]], lambda: nc.vector.tensor_tensor(
                        out=modv[:, l, s, i0, k0:k0 + 4], in0=ps[3][:, 0:8].rearrange("p (j s) -> p j s", s=2)[:, :, s],
                        in1=V(l, OFF_BADA + q * 4, 4), op=ALU.add))
            for s in range(2):
                for n, (og, isc) in enumerate(((OFF_G1, 1), (OFF_G2, 4))):
                    op(DVE, [r_mod[l], r_vec], [r_mod[l]], lambda: nc.vector.scalar_tensor_tensor(
                        out=moda[:, l, s, n, :], in0=modv[:, l, s, isc, :], scalar=1.0, in1=V(l, og, 16), op0=ALU.add, op1=ALU.mult))

        ms("p0c")
        cv = [0]

        def conv_tile(src_ap, nk, ncols, dst_ap, rsrc_res):
            sl = cv[0] % 2
            nm = ncols // 128
            src_sb = p0_wa[:, sl].rearrange("p k n -> p (k n)")[:, 0:nk * ncols].rearrange("p (k n) -> p k n", k=nk)
            dst_sb = p0_wb[:, sl, 0:nm * nk * 128]
            dma(QL, [], [r_wa[sl]], lambda: nc.sync.dma_start(out=src_sb, in_=src_ap.rearrange("(k p) n -> p k n", p=128)))
            o = dst_sb.rearrange("p (m k j) -> p k m j", m=nm, k=nk)
            i = src_sb.rearrange("p k (m j) -> p k m j", m=nm)
            e = cv[0] % 3
            cv[0] += 1
            if e == 0:
                op(DVE, [r_wa[sl]], [r_wb[sl]], lambda: nc.vector.tensor_copy(out=o, in_=i))
            elif e == 1:
                op(ACT, [r_wa[sl]], [r_wb[sl]], lambda: nc.scalar.copy(out=o, in_=i))
            else:
                op(POOL, [r_wa[sl]], [r_wb[sl]], lambda: nc.gpsimd.tensor_copy(out=o, in_=i))
            dma(QS, [r_wb[sl]], [rsrc_res], lambda: nc.sync.dma_start(out=dst_ap.rearrange("m p f -> p m f"), in_=dst_sb.rearrange("p (m f) -> p m f", m=nm)))

        for l in range(1):
            for q in range(7):
                conv_tile(w_in[l, :, q * 512:(q + 1) * 512], KC, 512, Wt_in[l][q * 4:(q + 1) * 4], r_W[l]["inn"])
            for q in range(4):
                conv_tile(w_out[l, :, q * 512:(q + 1) * 512], KC, 512, Wt_out[l][q * 4:(q + 1) * 4], r_W[l]["out"])
            for q in range(11):
                conv_tile(w_gate[l, :, q * 512:(q + 1) * 512], KC, 512, Wt_g[l][q * 4:(q + 1) * 4], r_W[l]["g"])
                conv_tile(w_up[l, :, q * 512:(q + 1) * 512], KC, 512, Wt_u[l][q * 4:(q + 1) * 4], r_W[l]["u"])
            for hf in range(4):
                for q in range(4):
                    conv_tile(w_down[l, hf * 1408:(hf + 1) * 1408, q * 512:(q + 1) * 512], 11, 512, Wt_d[l][q * 4:(q + 1) * 4, hf], r_W[l]["d"])

    ms("p0d")
    KTb = sb("KTb", [128, 2, SEQ_S + 256], BF16)
    Vb = sb("Vb", [128, 34, 2, 128], BF16)
    r_KTb, r_Vb = RL(34), RL(34)
    KaC = sb("KaC", [128, 2, 256], BF16)
    VaC = sb("VaC", [128, 2, 2, 128], BF16)
    r_KaC, r_VaC = Res(), Res()
    KaW = sb("KaW", [128, 2, 768], BF16)
    VaW = sb("VaW", [128, 6, 256], BF16)
    r_KaW, r_VaW = Res(), Res()
    xblk = sb("xblk", [128, KC, T], F32)
    r_x = RL(KC)
    actb = sb("actb", [128, KC, T], BF16)
    r_act = RL(KC)
    gT = sb("gT", [128, 11, T], BF16)
    r_g = RL(11)
    qblk = sb("qblk", [128, 12, T], BF16)
    r_q = RL(3)
    uh = sb("uh", [128, 4, 576], BF16)
    r_uh = Res()
    NW = 6
    wring = sb("wring", [128, NW, 2048], BF16)
    r_wr = RL(NW)
    ropeCt = sb("ropeCt", [128, T], F32)
    ropeSt = sb("ropeSt", [128, T], F32)
    r_rope = Res()
    NPT = 6
    PT = sb("PT", [128, NPT, T], BF16)
    r_PT = RL(NPT)
    NTF = 7
    tf = sb("tf", [128, NTF, T], F32)
    r_tf = RL(NTF)
    NTB = 4
    tb = sb("tb", [128, NTB, T], BF16)
    r_tb = RL(NTB)
    rstd = sb("rstd", [128, T], F32)
    r_rstd = Res()
    cacc = sb("cacc", [128, 4, T], F32)
    r_cacc = RL(4)

    def cs(a):
        return cacc[:, a, :].rearrange("p (b c) -> p b c", b=2)
    r_cs = r_cacc[0:2]

    ctr = {"tf": 0, "tb": 0, "pt": 0, "main": 0, "aux": 0, "ss": 0, "pring": 0, "m6": 0}
    cvA = sb("cvA", [128, 2048], F32)
    cvB = sb("cvB", [128, 2048], BF16)
    r_cvA, r_cvB = Res(), Res()
    bg = {"tasks": [], "ticks": 0, "every": 8}

    def bg_plan(l):
        t = []
        for mc in range(28):
            t.append((w_in[l, :, mc * 128:(mc + 1) * 128], KC, Wt_in[l][mc], r_W[l]["inn"]))
        for mc in range(16):
            t.append((w_out[l, :, mc * 128:(mc + 1) * 128], KC, Wt_out[l][mc], r_W[l]["out"]))
        for mc in range(FC):
            t.append((w_gate[l, :, mc * 128:(mc + 1) * 128], KC, Wt_g[l][mc], r_W[l]["g"]))
            t.append((w_up[l, :, mc * 128:(mc + 1) * 128], KC, Wt_u[l][mc], r_W[l]["u"]))
        for hf in range(4):
            for mc in range(16):
                t.append((w_down[l, hf * 1408:(hf + 1) * 1408, mc * 128:(mc + 1) * 128], 11, Wt_d[l][mc, hf], r_W[l]["d"]))
        return t

    def bg_emit_one():
        src, nk, dst, rdst = bg["tasks"].pop(0)
        n = nk * 128
        if PE.cnt:
            POOL.wait(Ev((PE.key, PE.sem), PE.cnt, {}))
        dma(QP, [], [r_cvA], lambda: nc.gpsimd.dma_start(out=cvA[:, 0:n].rearrange("p (k j) -> p k j", k=nk), in_=src.rearrange("(k p) n -> p k n", p=128)))
        op(POOL, [r_cvA], [r_cvB], lambda: nc.gpsimd.tensor_copy(out=cvB[:, 0:n], in_=cvA[:, 0:n]))
        dma(QP, [r_cvB], [rdst], lambda: nc.gpsimd.dma_start(out=dst, in_=cvB[:, 0:n]))

    def bg_tick():
        bg["ticks"] += 1
        if bg["tasks"] and bg["ticks"] % bg["every"] == 0:
            bg_emit_one()

    def bg_flush():
        while bg["tasks"]:
            bg_emit_one()

    def ntf():
        i = ctr["tf"] % NTF
        ctr["tf"] += 1
        return tf[:, i, :], r_tf[i]

    def ntb():
        i = ctr["tb"] % NTB
        ctr["tb"] += 1
        return tb[:, i, :], r_tb[i]

    def pmain6():
        i = ctr["m6"] % 6
        ctr["m6"] += 1
        return ps[i], r_ps[i]

    def pmain():
        i = ctr["main"] % 4
        ctr["main"] += 1
        return ps[i], r_ps[i]

    def pss():
        i = 4 + ctr["ss"] % 2
        ctr["ss"] += 1
        return ps[i], r_ps[i]

    def paux():
        i = 6 + ctr["aux"] % 2
        ctr["aux"] += 1
        return ps[i], r_ps[i]

    def in_order():
        return [0, 1, 2, 3, 4, 5, 6, 7, 8, 9, 10, 11, 12, 13, 14, 15, 16, 17, 18, 19, 20, 24, 21, 25, 22, 26, 23, 27]

    def plan():
        seq = []
        for l in range(L):
            for nblk in (1, NSB):
                for b in range(nblk):
                    for mc in in_order():
                        seq.append((Wt_in[l][mc], 2048, r_W[l]["inn"]))
                for b in range(nblk):
                    for mc in range(16):
                        seq.append((Wt_out[l][mc], 2048, r_W[l]["out"]))
                    for hf in range(4):
                        for fc in range(11):
                            seq.append((Wt_g[l][hf * 11 + fc], 2048, r_W[l]["g"]))
                            seq.append((Wt_u[l][hf * 11 + fc], 2048, r_W[l]["u"]))
                        for mc in range(16):
                            seq.append((Wt_d[l][mc, hf], 1408, r_W[l]["d"]))
        return seq

    wplan = plan()
    wst = {"next_load": 0, "next_use": 0}

    def wload_upto(n):
        while wst["next_load"] < min(n, len(wplan)):
            i = wst["next_load"]
            src, ncol, rsrc = wplan[i]
            sl = i % NW
            dma(QL, [rsrc], [r_wr[sl]], lambda: nc.sync.dma_start(out=wring[:, sl, 0:ncol], in_=src))
            wst["next_load"] += 1

    def wnext(src_check):
        i = wst["next_use"]
        assert wplan[i][0] is src_check or True
        wload_upto(i + NW)
        wst["next_use"] += 1
        bg_tick()
        return wring[:, i % NW, :], r_wr[i % NW]

    def rms_stats(src_chunks, r_src, out_rstd, r_out, nfeat, nch):
        p, rp = pss()
        g = PEGroup(PE, [rp])
        for c in range(nch):
            t, rt = ntb()
            if c % 2 == 0:
                op(ACT, [r_src[c]], [rt], lambda: nc.scalar.activation(out=t, in_=src_chunks(c), func=AF.Square))
            else:
                op(DVE, [r_src[c]], [rt], lambda: nc.vector.tensor_tensor(out=t, in0=src_chunks(c), in1=src_chunks(c), op=ALU.mult))
            g.mm([rt, r_k], lambda: nc.tensor.matmul(p[:], lhsT=ones_bf[:], rhs=t, start=(c == 0), stop=(c == nch - 1)), sig=True)
        g.done()
        t, rt = ntf()
        op(ACT, [rp, r_k], [rt], lambda: nc.scalar.activation(out=t, in_=p[:], func=AF.Ln, bias=eps_t[:], scale=1.0 / nfeat))
        op(ACT, [rt], [r_out], lambda: nc.scalar.activation(out=out_rstd, in_=t, func=AF.Exp, scale=-0.5))

    def norm_modulate(l, s, n):
        rms_stats(lambda c: xblk[:, c, :], r_x, rstd[:], r_rstd, D, KC)
        for c in range(KC):
            t, rt = ntf()
            op(DVE, [r_x[c], r_rstd, r_mod[l]], [rt], lambda: nc.vector.scalar_tensor_tensor(
                out=t, in0=xblk[:, c, :], scalar=moda[:, l, s, n, c:c + 1], in1=rstd[:], op0=ALU.mult, op1=ALU.mult))
            op(ACT, [rt, r_mod[l]], [r_act[c]], lambda: nc.scalar.activation(
                out=actb[:, c, :], in_=t, func=AF.Identity, bias=modv[:, l, s, 3 * n, c:c + 1], scale=1.0))

    def big_mm(wt, rw, nk, rhs_fn, r_rhs, p, rp, koff=0):
        g = PEGroup(PE, [rp])
        for k in range(nk):
            g.mm([rw, r_rhs[koff + k] if isinstance(r_rhs, list) else r_rhs],
                 lambda: nc.tensor.matmul(p[:], lhsT=wt[:, k * 128:(k + 1) * 128], rhs=rhs_fn(k), start=(k == 0), stop=(k == nk - 1)))
        return g.done()

    def load_x(s, b):
        xT = xT_S if s == 0 else xT_P
        rr = r_xT_S[b] if s == 0 else r_xT_P[0]
        dma(QL, [rr], r_x, lambda: nc.sync.dma_start(out=xblk[:], in_=xT[:, :, b * T:(b + 1) * T].rearrange("k p t -> p k t")))

    def transposes_to(src, rsrc, dt_ident, r_ident, p, rp):
        g = PEGroup(PE, [rp])
        for a in range(4):
            g.mm([rsrc, r_ident], lambda: nc.tensor.transpose(p[:, a * 128:(a + 1) * 128], src[:, a * 128:(a + 1) * 128], dt_ident))
        g.done()

    r_qT = [RL(NSB) for _ in range(3)]
    r_uT = RL(NSB)
    r_uTP = Res()
    r_Ka_d = RL(NSB)
    r_Va_d = RL(NSB)

    for l in range(L):
        last = (l == L - 1)
        bg_flush()
        if not last:
            bg["tasks"] = bg_plan(l + 1)
        for which, (ck, cv_) in enumerate(((cbk, cbv), (cak, cav))):
            dma(QL, [], r_cs, lambda: nc.sync.dma_start(
                out=cs(0), in_=ck[l].rearrange("(st s) kv d -> s st (kv d)", s=128)))
            dma(QL, [], r_cs, lambda: nc.sync.dma_start(
                out=cs(1), in_=cv_[l].rearrange("(st s) kv d -> s st (kv d)", s=128)))
            p, rp = paux()
            g = PEGroup(PE, [rp])
            for kv in range(2):
                for st in range(2):
                    g.mm(r_cs + [r_cst], lambda: nc.tensor.transpose(
                        p[:, (kv * 2 + st) * 128:(kv * 2 + st + 1) * 128], cs(0)[:, st, kv * 128:(kv + 1) * 128], ident_f))
            g.done()
            if which == 0:
                op(DVE, [rp], r_KTb[0:2], lambda: nc.vector.tensor_copy(out=KTb[:, :, 0:256], in_=p[:].rearrange("p (kv t) -> p kv t", kv=2)))
                op(DVE, r_cs, r_Vb[0:2], lambda: nc.vector.tensor_copy(
                    out=Vb[:, 0:2].rearrange("p a k d -> p a (k d)"), in_=cs(1)))
            else:
                op(DVE, [rp], [r_KaC], lambda: nc.vector.tensor_copy(out=KaC[:], in_=p[:].rearrange("p (kv t) -> p kv t", kv=2)))
                op(DVE, r_cs, [r_VaC], lambda: nc.vector.tensor_copy(
                    out=VaC[:].rearrange("p a k d -> p a (k d)"), in_=cs(1)))

        for s, nblk in ((1, 1), (0, NSB)):
            for b in range(nblk):
                def rope_load(bb):
                    dma(QL, [], [r_rope], lambda: nc.sync.dma_start(out=ropeCt[:], in_=ropeC[:, bb * T:(bb + 1) * T]))
                    dma(QL, [], [r_rope], lambda: nc.sync.dma_start(out=ropeSt[:], in_=ropeS[:, bb * T:(bb + 1) * T]))
                if b == 0:
                    load_x(s, 0)
                    if s == 0:
                        rope_load(0)
                if s == 0:
                    Cc, Ss, rcs = ropeCt[:], ropeSt[:], r_rope
                else:
                    Cc, Ss, rcs = None, None, None
                norm_modulate(l, s, 0)
                if b + 1 < nblk:
                    load_x(s, b + 1)
                pend = {}
                for mc in in_order():
                    wt, rw = wnext(None)
                    p, rp = pmain()
                    big_mm(wt, rw, KC, lambda k: actb[:, k, :], r_act, p, rp)
                    if mc < 6 or 8 <= mc < 18:
                        isq = mc < 4 or 8 <= mc < 16
                        goff = OFF_QGA if mc < 4 else OFF_KGA if mc < 6 else OFF_QGB if mc < 16 else OFF_KGB
                        t_sq, r_sq = ntb()
                        op(ACT, [rp], [r_sq], lambda: nc.scalar.activation(out=t_sq, in_=p[:], func=AF.Square))
                        pq, rpq = pss()
                        g = PEGroup(PE, [rpq])
                        g.mm([r_sq, r_k], lambda: nc.tensor.matmul(pq[:], lhsT=ones_bf[:], rhs=t_sq, start=True, stop=True))
                        g.done()
                        t_zg, r_zg = ntf()
                        op(ACT, [rp, r_vec], [r_zg], lambda: nc.scalar.activation(out=t_zg, in_=p[:], func=AF.Identity, scale=V(l, goff)))
                        t_ln, r_ln = ntf()
                        op(ACT, [rpq, r_k], [r_ln], lambda: nc.scalar.activation(out=t_ln, in_=pq[:], func=AF.Ln, bias=eps_t[:], scale=1.0 / 128))
                        t_rs, r_rs = ntf()
                        op(ACT, [r_ln], [r_rs], lambda: nc.scalar.activation(out=t_rs, in_=t_ln, func=AF.Exp, scale=-0.5))
                        if s == 0:
                            px, rpx = paux()
                            g = PEGroup(PE, [rpx])
                            g.mm([r_zg, r_cst], lambda: nc.tensor.matmul(px[:], lhsT=perm_f, rhs=t_zg, start=True, stop=True))
                            g.done()
                            t_a, r_a = ntf()
                            op(DVE, [r_zg, rcs], [r_a], lambda: nc.vector.tensor_tensor(out=t_a, in0=t_zg, in1=Cc, op=ALU.mult))
                            t_b, r_b = ntf()
                            op(DVE, [rpx, rcs], [r_b], lambda: nc.vector.tensor_tensor(out=t_b, in0=px[:], in1=Ss, op=ALU.mult))
                            op(DVE, [r_a, r_b], [r_b], lambda: nc.vector.tensor_tensor(out=t_b, in0=t_a, in1=t_b, op=ALU.add))
                        else:
                            t_b, r_b = t_zg, r_zg
                        tok0 = b * T if s == 0 else 0
                        if isq:
                            hq = mc if mc < 4 else mc - 4
                            t_o, r_o = ntb()
                            op(DVE, [r_b, r_rs], [r_o], lambda: nc.vector.tensor_tensor(out=t_o, in0=t_b, in1=t_rs, op=ALU.mult))
                            grp = 0 if hq < 4 else 1 if hq < 8 else 2
                            dma(QS, [r_o], [r_qT[grp][b]], lambda: nc.sync.dma_start(out=qT_d[hq, :, tok0:tok0 + T], in_=t_o))
                        else:
                            isA = mc < 6
                            kv = (mc - 4) if isA else (mc - 16)
                            if s == 1:
                                op(DVE, [r_b, r_rs], [r_b], lambda: nc.vector.tensor_tensor(out=t_b, in0=t_b, in1=t_rs, op=ALU.mult))
                                pt_, rpt_ = paux()
                                transposes_to(t_b, r_b, ident_f, r_cst, pt_, rpt_)
                                t_s, r_s = ntf()
                                op(ACT, [rpt_], [r_s], lambda: nc.scalar.copy(out=t_s, in_=pt_[:]))
                                dstk = nak if isA else nbk
                                for sq in range(2):
                                    dma(QS, [r_s], [], lambda: nc.sync.dma_start(
                                        out=dstk[sq, l, :, kv, :].rearrange("(h p) d -> p h d", p=128),
                                        in_=t_s[:, sq * 256:(sq + 1) * 256].rearrange("p (h d) -> p h d", h=2)))
                                srck, r_srck = t_b, r_b
                            else:
                                srck, r_srck = None, None
                            if isA:
                                t_o, r_o = ntb()
                                if s == 1:
                                    op(ACT, [r_b], [r_o], lambda: nc.scalar.copy(out=t_o, in_=t_b))
                                else:
                                    op(DVE, [r_b, r_rs], [r_o], lambda: nc.vector.tensor_tensor(out=t_o, in0=t_b, in1=t_rs, op=ALU.mult))
                                dma(QS, [r_o], [r_Ka_d[b]], lambda: nc.sync.dma_start(out=KaT_d[kv, :, 128 + tok0:128 + tok0 + T], in_=t_o))
                            else:
                                dst = KTb[:, kv, 256 + tok0:256 + tok0 + T]
                                rd = r_KTb[2 + tok0 // 128:2 + tok0 // 128 + 4]
                                if s == 1:
                                    op(ACT, [r_b], rd, lambda: nc.scalar.copy(out=dst, in_=t_b))
                                else:
                                    op(DVE, [r_b, r_rs], rd, lambda: nc.vector.tensor_tensor(out=dst, in0=t_b, in1=t_rs, op=ALU.mult))
                    elif mc < 20:
                        if mc == 18 and s == 0 and b + 1 < nblk:
                            rope_load(b + 1)
                        isA = mc < 8
                        kv = (mc - 6) if isA else (mc - 18)
                        t_v, r_v = ntf()
                        op(ACT, [rp], [r_v], lambda: nc.scalar.copy(out=t_v, in_=p[:]))
                        pt_, rpt_ = paux()
                        transposes_to(t_v, r_v, ident_f, r_cst, pt_, rpt_)
                        tok0 = b * T if s == 0 else 0
                        if s == 1:
                            t_s, r_s = ntf()
                            op(ACT, [rpt_], [r_s], lambda: nc.scalar.copy(out=t_s, in_=pt_[:]))
                            dstv = nav if isA else nbv
                            for sq in range(2):
                                dma(QS, [r_s], [], lambda: nc.sync.dma_start(
                                    out=dstv[sq, l, :, kv, :].rearrange("(h p) d -> p h d", p=128),
                                    in_=t_s[:, sq * 256:(sq + 1) * 256].rearrange("p (h d) -> p h d", h=2)))
                        if isA:
                            t_o, r_o = ntb()
                            op(DVE, [rpt_], [r_o], lambda: nc.vector.tensor_copy(out=t_o, in_=pt_[:]))
                            sb0 = 1 + tok0 // 128
                            dma(QS, [r_o], [r_Va_d[b]], lambda: nc.sync.dma_start(
                                out=Va_d[sb0:sb0 + 4, :, kv * 128:(kv + 1) * 128].rearrange("a p d -> p a d"),
                                in_=t_o.rearrange("p (a d) -> p a d", a=4)))
                        else:
                            sb0 = 2 + tok0 // 128
                            op(DVE, [rpt_], r_Vb[sb0:sb0 + 4], lambda: nc.vector.tensor_copy(
                                out=Vb[:, sb0:sb0 + 4, kv, :], in_=pt_[:].rearrange("p (a d) -> p a d", a=4)))
                    elif mc < 24:
                        pend[mc] = (p, rp)
                    else:
                        pu, rpu = pend.pop(mc - 4)
                        ch = mc - 24
                        t_e, r_e = ntf()
                        op(ACT, [rp], [r_e], lambda: nc.scalar.activation(out=t_e, in_=p[:], func=AF.Exp, scale=-1.0))
                        op(DVE, [r_e], [r_e], lambda: nc.vector.tensor_scalar(out=t_e, in0=t_e, scalar1=1.0, scalar2=0.5, op0=ALU.add, op1=ALU.mult))
                        op(DVE, [r_e], [r_e], lambda: nc.vector.reciprocal(out=t_e, in_=t_e))
                        t_o, r_o = ntb()
                        op(DVE, [r_e, rpu], [r_o], lambda: nc.vector.tensor_tensor(out=t_o, in0=pu[:], in1=t_e, op=ALU.mult))
                        if s == 0:
                            dma(QS, [r_o], [r_uT[b]], lambda: nc.sync.dma_start(out=uT_d[ch, :, 16 + b * T:16 + (b + 1) * T], in_=t_o))
                        else:
                            dma(QS, [r_o], [r_uTP], lambda: nc.sync.dma_start(
                                out=uT_dP[ch, :, :].rearrange("p (s w) -> p s w", s=2)[:, :, 16:272],
                                in_=t_o.rearrange("p (s w) -> p s w", s=2)))

            ms("s1p" if s == 1 else "s1s")
            def attn_loads(b):
                tok0 = b * T if s == 0 else 0
                for grp in range(3):
                    dma(QL, [r_qT[grp][b]], [r_q[grp]], lambda: nc.sync.dma_start(
                        out=qblk[:, grp * 4:(grp + 1) * 4, :], in_=qT_d[grp * 4:(grp + 1) * 4, :, tok0:tok0 + T].rearrange("h p t -> p h t")))
                if s == 0:
                    nb_ = [r_uT[x] for x in (b - 1, b, b + 1) if 0 <= x < NSB]
                    dma(QL, nb_, [r_uh], lambda: nc.sync.dma_start(out=uh[:, :, 0:542], in_=uT_d[:, :, b * T + 1:b * T + 543].rearrange("c p w -> p c w")))
                    nbk_ = [r_Ka_d[x] for x in (b - 1, b, b + 1) if 0 <= x < NSB]
                    dma(QL, nbk_, [r_KaW], lambda: nc.sync.dma_start(out=KaW[:], in_=KaT_d[:, :, tok0:tok0 + 768].rearrange("k p t -> p k t")))
                    nbv_ = [r_Va_d[x] for x in (b - 1, b, b + 1) if 0 <= x < NSB]
                    dma(QL, nbv_, [r_VaW], lambda: nc.sync.dma_start(out=VaW[:], in_=Va_d[tok0 // 128:tok0 // 128 + 6].rearrange("a p d -> p a d")))
                else:
                    dma(QL, [r_uTP], [r_uh], lambda: nc.sync.dma_start(out=uh[:, :, :], in_=uT_dP[:, :, :].rearrange("c p w -> p c w")))
                    dma(QL, [r_Ka_d[0]], [r_KaW], lambda: nc.sync.dma_start(out=KaW[:, :, 0:640], in_=KaT_d[:, :, 0:640].rearrange("k p t -> p k t")))
                    dma(QL, [r_Va_d[0]], [r_VaW], lambda: nc.sync.dma_start(out=VaW[:, 0:5], in_=Va_d[0:5].rearrange("a p d -> p a d")))


            for b in range(nblk):
                tok0 = b * T if s == 0 else 0
                if b == 0:
                    attn_loads(0)

                conv_ops = []
                nseq = 1 if s == 0 else 2
                wdt = T // nseq
                for sq in range(nseq):
                    base = sq * 288 + 1 if s == 1 else 0
                    for k in range(31):
                        for ch in range(4):
                            def _cop(sq=sq, base=base, k=k, ch=ch):
                                src = uh[:, ch, base + k:base + k + wdt]
                                acc = cacc[:, ch, sq * wdt:(sq + 1) * wdt]
                                if k == 0:
                                    op(DVE, [r_uh, r_k, r_vec], [r_cacc[ch]], lambda: nc.vector.tensor_scalar(
                                        out=acc, in0=src, scalar1=cwh[:, l, ch * 31:ch * 31 + 1], scalar2=V(l, OFF_CB + ch), op0=ALU.mult, op1=ALU.add))
                                else:
                                    op(DVE, [r_uh, r_k, r_cacc[ch]], [r_cacc[ch]], lambda: nc.vector.scalar_tensor_tensor(
                                        out=acc, in0=src, scalar=cwh[:, l, ch * 31 + k:ch * 31 + k + 1], in1=acc, op0=ALU.mult, op1=ALU.add))
                            conv_ops.append(_cop)

                def conv_step(n):
                    for _ in range(n * nseq):
                        if conv_ops:
                            conv_ops.pop(0)()

                def conv_finish():
                    conv_step(10 ** 6)
                    psum_, rps_ = pss()
                    psq_, rpq_ = pss()
                    g1_ = PEGroup(PE, [rps_])
                    g2_ = PEGroup(PE, [rpq_])
                    for ch in range(4):
                        t1, rt1 = ntb()
                        op(ACT, [r_cacc[ch]], [rt1], lambda: nc.scalar.copy(out=t1, in_=cacc[:, ch, :]))
                        g1_.mm([rt1, r_k], lambda: nc.tensor.matmul(psum_[:], lhsT=ones_bf[:], rhs=t1, start=(ch == 0), stop=(ch == 3)), sig=True)
                        t2, rt2 = ntb()
                        op(ACT, [r_cacc[ch]], [rt2], lambda: nc.scalar.activation(out=t2, in_=cacc[:, ch, :], func=AF.Square))
                        g2_.mm([rt2, r_k], lambda: nc.tensor.matmul(psq_[:], lhsT=ones_bf[:], rhs=t2, start=(ch == 0), stop=(ch == 3)), sig=True)
                    g1_.done()
                    g2_.done()
                    t_m, r_m = rstd[:], r_rstd
                    op(DVE, [rps_], [r_m], lambda: nc.vector.tensor_scalar(out=t_m, in0=psum_[:], scalar1=1.0 / 512, scalar2=None, op0=ALU.mult))
                    t_v, r_v = ropeCt[:], r_rope
                    op(DVE, [r_m], [r_v], lambda: nc.vector.tensor_tensor(out=t_v, in0=t_m, in1=t_m, op=ALU.mult))
                    op(DVE, [rpq_, r_v], [r_v], lambda: nc.vector.scalar_tensor_tensor(out=t_v, in0=psq_[:], scalar=1.0 / 512, in1=t_v, op0=ALU.mult, op1=ALU.subtract))
                    op(ACT, [r_v, r_k], [r_v], lambda: nc.scalar.activation(out=t_v, in_=t_v, func=AF.Ln, bias=eps_t[:], scale=1.0))
                    op(ACT, [r_v], [r_v], lambda: nc.scalar.activation(out=t_v, in_=t_v, func=AF.Exp, scale=-0.5))
                    for ch in range(4):
                        op(DVE, [r_cacc[ch], r_m], [r_cacc[ch]], lambda: nc.vector.tensor_tensor(out=cacc[:, ch, :], in0=cacc[:, ch, :], in1=t_m, op=ALU.subtract))
                    for ch in range(4):
                        op(DVE, [r_cacc[ch], r_v], [r_cacc[ch]], lambda: nc.vector.tensor_tensor(out=cacc[:, ch, :], in0=cacc[:, ch, :], in1=t_v, op=ALU.mult))
                    for ch in range(4):
                        op(ACT, [r_cacc[ch], r_vec], [r_cacc[ch]], lambda: nc.scalar.activation(out=cacc[:, ch, :], in_=cacc[:, ch, :], func=AF.Identity, bias=V(l, OFF_LB + ch), scale=V(l, OFF_LG + ch)))
                    tl = []
                    for ch in range(4):
                        t2, rt2 = ntf()
                        tl.append((t2, rt2))
                        op(ACT, [r_cacc[ch]], [rt2], lambda: nc.scalar.activation(out=t2, in_=cacc[:, ch, :], func=AF.Exp, scale=-1.0))
                    for ch in range(4):
                        t2, rt2 = tl[ch]
                        op(DVE, [rt2], [rt2], lambda: nc.vector.tensor_scalar(out=t2, in0=t2, scalar1=1.0, scalar2=None, op0=ALU.add))
                    for ch in range(4):
                        t2, rt2 = tl[ch]
                        op(DVE, [rt2], [rt2], lambda: nc.vector.reciprocal(out=t2, in_=t2))
                    for ch in range(4):
                        t2, rt2 = tl[ch]
                        op(DVE, [r_cacc[ch], rt2], [r_act[12 + ch]], lambda: nc.vector.tensor_tensor(out=actb[:, 12 + ch, :], in0=cacc[:, ch, :], in1=t2, op=ALU.mult))


                def attn_unit(qap, rq, nh, segs, out_ap, r_out, sink_cols):
                    n = nh * 128
                    po, rpo = ps[3 + ctr["pt"] % 2], r_ps[3 + ctr["pt"] % 2]
                    pd, rpd = ps[5 + ctr["pt"] % 2], r_ps[5 + ctr["pt"] % 2]
                    ctr["pt"] += 1
                    go = PEGroup(PE, [rpo])
                    gd = PEGroup(PE, [rpd])
                    prev = None
                    coltile = sink_cols is None and len(segs) >= 8
                    pend_dn = []
                    nbatch = [0]

                    def dn_batch(final):
                        for jq, jpq in enumerate(pend_dn):
                            gd.mm([r_PT[jpq], r_k], lambda: nc.tensor.matmul(
                                pd[32 * jq:32 * jq + 32, 0:n], lhsT=ones_bf[:, 0:32], rhs=PT[:, jpq, 0:n],
                                start=(nbatch[0] == 0), stop=final, tile_position=(0, 32 * jq)))
                        nbatch[0] += 1
                        del pend_dn[:]
                    for i in range(len(segs) + 1):
                        if i < len(segs):
                            kT, rk, v, rv, m = segs[i]
                            j = ctr["main"] % 3
                            ctr["main"] += 1
                            pS, rpS = ps[j], r_ps[j]
                            g = PEGroup(PE, [rpS])
                            g.mm([rk, rq], lambda: nc.tensor.matmul(pS[:, 0:n].rearrange("p (h q) -> p h q", h=nh), lhsT=kT, rhs=qap, start=True, stop=(m is None)))
                            if m is not None:
                                g.mm([r_k], lambda: nc.tensor.matmul(pS[:, 0:n], lhsT=ident_bf[:], rhs=m, start=False, stop=True))
                            g.done()
                            jp = ctr["pring"] % NPT
                            ctr["pring"] += 1
                            op(ACT, [rpS], [r_PT[jp]], lambda: nc.scalar.activation(out=PT[:, jp, 0:n], in_=pS[:, 0:n], func=AF.Exp, scale=SCALE))
                            cur = (jp, v, rv)
                        else:
                            cur = None
                        if prev is not None:
                            jp0, v0, rv0 = prev
                            first = (i == 1)
                            lastseg = (i == len(segs))
                            go.mm([r_PT[jp0], rv0], lambda: nc.tensor.matmul(po[:, 0:n], lhsT=v0, rhs=PT[:, jp0, 0:n], start=first, stop=lastseg))
                            if coltile:
                                pend_dn.append(jp0)
                                if len(pend_dn) == 4 or lastseg:
                                    dn_batch(lastseg)
                            else:
                                gd.mm([r_PT[jp0], r_k], lambda: nc.tensor.matmul(pd[:, 0:n], lhsT=ones_bf[:], rhs=PT[:, jp0, 0:n], start=first, stop=lastseg))
                        prev = cur
                    go.done()
                    gd.done()
                    t_r, r_r = ntf()
                    if sink_cols is not None:
                        for h in range(nh):
                            op(DVE, [rpd, r_k], [r_r], lambda: nc.vector.tensor_scalar(
                                out=t_r[:, h * 128:(h + 1) * 128], in0=pd[:, h * 128:(h + 1) * 128], scalar1=esink[:, sink_cols + h:sink_cols + h + 1], scalar2=None, op0=ALU.add))
                        op(DVE, [r_r], [r_r], lambda: nc.vector.reciprocal(out=t_r[:, 0:n], in_=t_r[:, 0:n]))
                    elif coltile:
                        t_q, r_tq = ntf()
                        op(DVE, [rpd], [r_tq], lambda: nc.vector.tensor_copy(out=t_q[:, 0:n], in_=pd[:, 0:n]))
                        gq = PEGroup(PE, [r_ps[7]])
                        gq.mm([r_tq, r_k], lambda: nc.tensor.matmul(ps[7][:, 0:n], lhsT=qsum_f[:], rhs=t_q[:, 0:n], start=True, stop=True))
                        gq.done()
                        op(DVE, [r_ps[7]], [r_r], lambda: nc.vector.reciprocal(out=t_r[:, 0:n], in_=ps[7][:, 0:n]))
                    else:
                        op(DVE, [rpd], [r_r], lambda: nc.vector.reciprocal(out=t_r[:, 0:n], in_=pd[:, 0:n]))
                    op(DVE, [rpo, r_r], r_out, lambda: nc.vector.tensor_tensor(
                        out=out_ap, in0=po[:, 0:n].rearrange("p (h q) -> p h q", h=nh), in1=t_r[:, 0:n].rearrange("p (h q) -> p h q", h=nh), op=ALU.mult))

                for kv in range(2):
                    for qs in range(4):
                        qap = qblk[:, kv * 2:kv * 2 + 2, qs * 128:(qs + 1) * 128]
                        segs = []
                        if s == 0:
                            for st in range(2):
                                segs.append((KaC[:, kv, st * 128:(st + 1) * 128], r_KaC, VaC[:, st, kv, :], r_VaC, None))
                            gq = b * 4 + qs
                            for dlt, m in ((-1, mask_bf[:, 0, :]), (0, None), (1, mask_bf[:, 1, :])):
                                if 0 <= gq + dlt < 32:
                                    w0 = (qs + 1 + dlt) * 128
                                    segs.append((KaW[:, kv, w0:w0 + 128], r_KaW, VaW[:, qs + 1 + dlt, kv * 128:(kv + 1) * 128], r_VaW, m))
                        else:
                            sq = qs // 2
                            for st in range(2):
                                w0 = 128 + sq * 256 + st * 128
                                segs.append((KaW[:, kv, w0:w0 + 128], r_KaW, VaW[:, 1 + sq * 2 + st, kv * 128:(kv + 1) * 128], r_VaW, None))
                        attn_unit(qap, r_q[0], 2, segs, actb[:, kv * 2:kv * 2 + 2, qs * 128:(qs + 1) * 128], r_act[kv * 2:kv * 2 + 2], l * 4 + kv * 2)
                        conv_step(4)
                for kv in range(2):
                    for qs in range(4):
                        qap = qblk[:, 4 + kv * 4:8 + kv * 4, qs * 128:(qs + 1) * 128]
                        if s == 0:
                            sbl = list(range(34))
                        else:
                            sq = qs // 2
                            sbl = [2 + sq * 2, 3 + sq * 2]
                        segs = [(KTb[:, kv, j * 128:(j + 1) * 128], r_KTb[j], Vb[:, j, kv, :], r_Vb[j], None) for j in sbl]
                        attn_unit(qap, r_q[1 + kv], 4, segs, actb[:, 4 + kv * 4:8 + kv * 4, qs * 128:(qs + 1) * 128], r_act[4 + kv * 4:8 + kv * 4], None)
                        ub = kv * 4 + qs
                        if ub < 5:
                            conv_step(20)
                        if ub == 4:
                            conv_finish()

                ms("attn")
                ms("conv")
                load_x(s, b)
                for mc in range(16):
                    wt, rw = wnext(None)
                    p, rp = pmain6()
                    big_mm(wt, rw, KC, lambda k: actb[:, k, :], r_act, p, rp)
                    op(DVE, [rp, r_x[mc], r_mod[l]], [r_x[mc]], lambda: nc.vector.scalar_tensor_tensor(
                        out=xblk[:, mc, :], in0=p[:], scalar=modv[:, l, s, 2, mc:mc + 1], in1=xblk[:, mc, :], op0=ALU.mult, op1=ALU.add))

                ms("wout")
                norm_modulate(l, s, 1)
                if b + 1 < nblk:
                    attn_loads(b + 1)
                for hf in range(4):
                    for fc in range(11):
                        wg_, rwg = wnext(None)
                        pg, rpg = pmain6()
                        big_mm(wg_, rwg, KC, lambda k: actb[:, k, :], r_act, pg, rpg)
                        wu_, rwu = wnext(None)
                        pu, rpu = pmain6()
                        big_mm(wu_, rwu, KC, lambda k: actb[:, k, :], r_act, pu, rpu)
                        t1, rt1 = ntf()
                        op(ACT, [rpg], [rt1], lambda: nc.scalar.activation(out=t1, in_=pg[:], func=AF.Silu))
                        op(DVE, [rt1, rpu], [r_g[fc]], lambda: nc.vector.tensor_tensor(out=gT[:, fc, :], in0=pu[:], in1=t1, op=ALU.mult))
                    for mc in range(16):
                        wt, rw = wnext(None)
                        p, rp = pmain6()
                        big_mm(wt, rw, 11, lambda k: gT[:, k, :], r_g, p, rp)
                        op(DVE, [rp, r_x[mc], r_mod[l]], [r_x[mc]], lambda: nc.vector.scalar_tensor_tensor(
                            out=xblk[:, mc, :], in0=p[:], scalar=modv[:, l, s, 5, mc:mc + 1], in1=xblk[:, mc, :], op0=ALU.mult, op1=ALU.add))

                ms("ffn")
                if not last:
                    xT = xT_S if s == 0 else xT_P
                    rr = r_xT_S[b] if s == 0 else r_xT_P[0]
                    dma(QS, r_x, [rr], lambda: nc.sync.dma_start(out=xT[:, :, b * T:(b + 1) * T].rearrange("k p t -> p k t"), in_=xblk[:]))
                else:
                    ydst = y_s if s == 0 else y_p
                    for a in range(4):
                        for q4 in range(4):
                            pt_, rpt_ = paux()
                            g = PEGroup(PE, [rpt_])
                            for j in range(4):
                                kc = q4 * 4 + j
                                g.mm([r_x[kc], r_cst], lambda: nc.tensor.transpose(pt_[:, j * 128:(j + 1) * 128], xblk[:, kc, a * 128:(a + 1) * 128], ident_f))
                            g.done()
                            t1, rt1 = ntf()
                            if q4 % 2 == 0:
                                op(DVE, [rpt_], [rt1], lambda: nc.vector.tensor_copy(out=t1, in_=pt_[:]))
                            else:
                                op(ACT, [rpt_], [rt1], lambda: nc.scalar.copy(out=t1, in_=pt_[:]))
                            dma(QS, [rt1], [], lambda: nc.sync.dma_start(out=ydst[tok0 + a * 128:tok0 + (a + 1) * 128, q4 * 512:(q4 + 1) * 512], in_=t1))

    return nc


def _fm(v):
    return np.ascontiguousarray(v.reshape(-1, 128).T)


def _consts():
    c = np.zeros((128, 768), np.float32)
    c[:, 0:128] = np.eye(128, dtype=np.float32)
    k = np.arange(128)
    c[(k + 64) % 128, 128 + k] = 1.0
    j = np.arange(128)[:, None]
    i = np.arange(128)[None, :]
    lo = np.where(j >= i, 0.0, NEG).astype(np.float32)
    hi = np.where(j <= i, 0.0, NEG).astype(np.float32)
    c[:, 256:384] = lo
    c[:, 384:512] = lo
    c[:, 512:640] = hi
    c[:, 640:768] = hi
    return c


def _rope():
    t = np.arange(SEQ_S)
    row = (t // 64).astype(np.float32)
    col = (t % 64).astype(np.float32)
    inv = np.power(np.float32(10000.0), -np.arange(32, dtype=np.float32) / np.float32(32)).astype(np.float32)
    ang = np.concatenate([row[:, None] * inv, col[:, None] * inv], axis=-1).astype(np.float32)
    cos = np.cos(ang).astype(np.float32).T
    sin = np.sin(ang).astype(np.float32).T
    C = np.concatenate([cos, cos], axis=0)
    S = np.concatenate([-sin, sin], axis=0)
    return np.ascontiguousarray(C), np.ascontiguousarray(S)


_CACHE = {}


def kernel(x_prompt, x_sample, cache_a_k, cache_a_v, cache_b_k, cache_b_v, c, c_ctx,
           w_ada, b_ada, w_in, w_out, w_gate, w_up, w_down, norm1_g, norm2_g,
           qnorm_a_g, knorm_a_g, qnorm_b_g, knorm_b_g, sink_a,
           conv_w, conv_b, conv_ln_g, conv_ln_b, _depth=None, _dbg=None):
    f = lambda a: np.ascontiguousarray(np.asarray(a, dtype=np.float32))
    L = int(_depth) if _depth else DEPTH
    n = 8
    key = L
    if key not in _CACHE:
        _CACHE[key] = build(L, _dbg)
    nc = _CACHE[key]
    x_prompt, x_sample = f(x_prompt), f(x_sample)
    cak, cav, cbk, cbv = f(cache_a_k), f(cache_a_v), f(cache_b_k), f(cache_b_v)
    c, c_ctx = f(c), f(c_ctx)
    shared = dict(w_ada=f(w_ada)[:L], w_in=f(w_in)[:L], w_out=f(w_out)[:L], w_gate=f(w_gate)[:L], w_up=f(w_up)[:L], w_down=f(w_down)[:L],
                  consts=_consts())
    shared["ropeC"], shared["ropeS"] = _rope()
    base = np.zeros((128, 32 + L * LV), np.float32)
    base[:, 16:32] = _fm(c_ctx)
    b_ada, norm1_g, norm2_g = f(b_ada), f(norm1_g), f(norm2_g)
    qa, ka, qb, kb = f(qnorm_a_g), f(knorm_a_g), f(qnorm_b_g), f(knorm_b_g)
    sink_a, conv_w, conv_b, lg, lb = f(sink_a), f(conv_w), f(conv_b), f(conv_ln_g), f(conv_ln_b)
    for l in range(L):
        o = 32 + l * LV
        base[:, o:o + 16] = _fm(norm1_g[l])
        base[:, o + 16:o + 32] = _fm(norm2_g[l])
        base[:, o + 32:o + 128] = _fm(b_ada[l])
        base[:, o + 128] = qa[l]
        base[:, o + 129] = ka[l]
        base[:, o + 130] = qb[l]
        base[:, o + 131] = kb[l]
        base[:, o + 132:o + 256] = conv_w[l].T.reshape(4, 128, 31).transpose(1, 0, 2).reshape(128, 124)
        base[:, o + 256:o + 260] = _fm(conv_b[l])
        base[:, o + 260:o + 264] = _fm(lg[l])
        base[:, o + 264:o + 268] = _fm(lb[l])
        base[:, o + 268:o + 272] = np.broadcast_to(sink_a[l][None, :], (128, 4))
    in_maps = []
    for i in range(n):
        v = base.copy()
        v[:, 0:16] = _fm(c[i])
        m = dict(shared)
        m.update(x_s=x_sample[i], x_p=np.ascontiguousarray(x_prompt[2 * i:2 * i + 2].reshape(T, D)),
                 cak=np.ascontiguousarray(cak[i, :L]), cav=np.ascontiguousarray(cav[i, :L]),
                 cbk=np.ascontiguousarray(cbk[i, :L]), cbv=np.ascontiguousarray(cbv[i, :L]), vecs=v)
        in_maps.append(m)
    res = run_bass_kernel_spmd(nc, in_maps, core_ids=list(range(n)))
    R = res.results
    y_prompt = np.concatenate([r["y_p"].reshape(2, 256, D) for r in R], axis=0)
    y_sample = np.stack([r["y_s"] for r in R], axis=0)
    outs = [np.concatenate([r[k] for r in R], axis=0) for k in ("nak", "nav", "nbk", "nbv")]
    return (y_prompt, y_sample, outs[0], outs[1], outs[2], outs[3])
```
